# Optimizing a Trainium2 kernel written in Bass

```python
import jax, jax.numpy as jnp
from jax import lax
import numpy as np

D_MODEL = 2048
BATCH = 32
SEQ = 256
DEPTH = 1
DEC_BATCH = 8
DEC_SEQ = 4096
PAST_LEN = 256

GRID_W = 64
HEAD_DIM = 128
N_HEADS = 8
N_KV_HEADS = 2
GROUPS = N_HEADS // N_KV_HEADS
Q_WIDTH = N_HEADS * HEAD_DIM
KV_WIDTH = N_KV_HEADS * HEAD_DIM
CONV_WIDTH = D_MODEL // 2
CONV_K = 3
D_FF = 4 * D_MODEL
Q_BLOCK = 128
ROPE_THETA = 10000.0
EPS = 1e-6
N_MOD = 6
IN_SIZES = (Q_WIDTH, KV_WIDTH, KV_WIDTH, CONV_WIDTH, CONV_WIDTH, CONV_WIDTH, D_MODEL, D_MODEL)
IN_WIDTH = sum(IN_SIZES)
IN_SPLITS = tuple(sum(IN_SIZES[:i + 1]) for i in range(len(IN_SIZES) - 1))

kernel_name = "hybrid_prefix_dit_gqa_shortconv_step"


def _rmsnorm(x, g):
    xf = x.astype(jnp.float32)
    y = xf * lax.rsqrt(jnp.mean(xf * xf, axis=-1, keepdims=True) + EPS)
    return (y * g.astype(jnp.float32)).astype(x.dtype)


def _axial_rope_tables(n_tokens, dtype):
    rows = n_tokens // GRID_W
    row = jnp.broadcast_to(jnp.arange(rows)[:, None], (rows, GRID_W)).reshape(-1).astype(jnp.float32)
    col = jnp.broadcast_to(jnp.arange(GRID_W)[None, :], (rows, GRID_W)).reshape(-1).astype(jnp.float32)
    half = HEAD_DIM // 2
    inv = ROPE_THETA ** (-jnp.arange(0, half, 2, dtype=jnp.float32) / half)
    ang = jnp.concatenate([row[:, None] * inv, col[:, None] * inv], axis=-1)
    return jnp.cos(ang).astype(dtype), jnp.sin(ang).astype(dtype)


def _rope(x, cos, sin):
    b, t, h, d = x.shape
    xp = x.reshape(b, t, h, d // 2, 2)
    x0, x1 = xp[..., 0], xp[..., 1]
    cs = cos[None, :, None, :]
    sn = sin[None, :, None, :]
    return jnp.stack([x0 * cs - x1 * sn, x0 * sn + x1 * cs], axis=-1).reshape(b, t, h, d)


def _short_conv(u, w):
    up = jnp.pad(u, ((0, 0), (1, 1), (0, 0)))
    return up[:, :-2] * w[0] + up[:, 1:-1] * w[1] + up[:, 2:] * w[2]


def _attention(q, k, v):
    b, t = q.shape[0], q.shape[1]
    nb = t // Q_BLOCK
    scale = HEAD_DIM ** -0.5
    qb = q.reshape(b, nb, Q_BLOCK, N_KV_HEADS, GROUPS, HEAD_DIM).transpose(1, 0, 2, 3, 4, 5)

    def block(qblk):
        s = jnp.einsum('bqkgd,bskd->bkgqs', qblk, k).astype(jnp.float32) * scale
        p = jax.nn.softmax(s, axis=-1).astype(v.dtype)
        return jnp.einsum('bkgqs,bskd->bqkgd', p, v)

    out = lax.map(block, qb)
    return out.transpose(1, 0, 2, 3, 4, 5).reshape(b, t, N_HEADS * HEAD_DIM)


def _modulation(cond, w_mod, b_mod):
    m = (jax.nn.silu(cond) @ w_mod + b_mod)[:, None, :]
    return jnp.split(m, N_MOD, axis=-1)


def _layer(x, mods, rope, ctx_kv, norm1_g, norm2_g, w_in, q_gain, k_gain, conv_w,
           w_attn_out, w_conv_out, w_o, w_ff1, w_ff2):
    b, t, _ = x.shape
    sh1, sc1, g1, sh2, sc2, g2 = mods
    h = _rmsnorm(x, norm1_g) * (1 + sc1) + sh1
    z = h @ w_in
    q, k, v, u, gate_c, gate_b, br_attn, br_conv = jnp.split(z, IN_SPLITS, axis=-1)
    q = _rmsnorm(q.reshape(b, t, N_HEADS, HEAD_DIM), q_gain)
    k = _rmsnorm(k.reshape(b, t, N_KV_HEADS, HEAD_DIM), k_gain)
    v = v.reshape(b, t, N_KV_HEADS, HEAD_DIM)
    own_kv = (k, v)
    if rope is not None:
        q = _rope(q, *rope)
        k = _rope(k, *rope)
    if ctx_kv is not None:
        k = jnp.concatenate([k, ctx_kv[0]], axis=1)
        v = jnp.concatenate([v, ctx_kv[1]], axis=1)
    attn = _attention(q, k, v)
    conv = gate_b * _short_conv(gate_c * u, conv_w)
    merged = jax.nn.sigmoid(br_attn) * (attn @ w_attn_out) + jax.nn.sigmoid(br_conv) * (conv @ w_conv_out)
    x = x + g1 * (merged @ w_o)
    h2 = _rmsnorm(x, norm2_g) * (1 + sc2) + sh2
    f = jnp.square(jax.nn.relu(h2 @ w_ff1)) @ w_ff2
    x = x + g2 * f
    return x, own_kv


def setup_inputs(seed: int = 0) -> dict:
    key = jax.random.key(seed)
    ks = jax.random.split(key, 24)

    def nrm(k, shape, fan_in):
        return jax.random.normal(k, shape, jnp.float32) * (fan_in ** -0.5)

    def gain(k, shape):
        return 1.0 + 0.01 * jax.random.normal(k, shape, jnp.float32)

    return {
        "x_prompt": jax.random.normal(ks[0], (BATCH, SEQ, D_MODEL), jnp.float32),
        "x_sample": jax.random.normal(ks[1], (DEC_BATCH, DEC_SEQ, D_MODEL), jnp.float32),
        "cache_k": jax.random.normal(ks[2], (DEC_BATCH, DEPTH, PAST_LEN, N_KV_HEADS, HEAD_DIM), jnp.float32),
        "cache_v": jax.random.normal(ks[3], (DEC_BATCH, DEPTH, PAST_LEN, N_KV_HEADS, HEAD_DIM), jnp.float32),
        "c": jax.random.normal(ks[4], (DEC_BATCH, D_MODEL), jnp.float32),
        "c_ctx": jax.random.normal(ks[5], (D_MODEL,), jnp.float32),
        "w_mod": nrm(ks[6], (DEPTH, D_MODEL, N_MOD * D_MODEL), D_MODEL),
        "b_mod": 0.02 * jax.random.normal(ks[7], (DEPTH, N_MOD * D_MODEL), jnp.float32),
        "norm1_g": gain(ks[8], (DEPTH, D_MODEL)),
        "norm2_g": gain(ks[9], (DEPTH, D_MODEL)),
        "w_in": nrm(ks[10], (DEPTH, D_MODEL, IN_WIDTH), D_MODEL),
        "q_gain": gain(ks[11], (DEPTH, HEAD_DIM)),
        "k_gain": gain(ks[12], (DEPTH, HEAD_DIM)),
        "conv_w": nrm(ks[13], (DEPTH, CONV_K, CONV_WIDTH), CONV_K),
        "w_attn_out": nrm(ks[14], (DEPTH, Q_WIDTH, D_MODEL), Q_WIDTH),
        "w_conv_out": nrm(ks[15], (DEPTH, CONV_WIDTH, D_MODEL), CONV_WIDTH),
        "w_o": nrm(ks[16], (DEPTH, D_MODEL, D_MODEL), D_MODEL),
        "w_ff1": nrm(ks[17], (DEPTH, D_MODEL, D_FF), D_MODEL),
        "w_ff2": nrm(ks[18], (DEPTH, D_FF, D_MODEL), D_FF),
        "final_g": gain(ks[19], (D_MODEL,)),
    }


def reference(x_prompt, x_sample, cache_k, cache_v, c, c_ctx, w_mod, b_mod, norm1_g, norm2_g,
              w_in, q_gain, k_gain, conv_w, w_attn_out, w_conv_out, w_o, w_ff1, w_ff2, final_g):
    rope = _axial_rope_tables(x_sample.shape[1], x_sample.dtype)
    xp = x_prompt
    xs = x_sample
    new_ks = []
    new_vs = []
    for l in range(DEPTH):
        weights = (norm1_g[l], norm2_g[l], w_in[l], q_gain[l], k_gain[l], conv_w[l],
                   w_attn_out[l], w_conv_out[l], w_o[l], w_ff1[l], w_ff2[l])
        mods_ctx = _modulation(c_ctx[None, :], w_mod[l], b_mod[l])
        xp, (k_ctx, v_ctx) = _layer(xp, mods_ctx, None, None, *weights)
        new_ks.append(k_ctx)
        new_vs.append(v_ctx)
        mods_lat = _modulation(c, w_mod[l], b_mod[l])
        xs, _ = _layer(xs, mods_lat, rope, (cache_k[:, l], cache_v[:, l]), *weights)
    y_prompt = _rmsnorm(xp, final_g)
    y_sample = _rmsnorm(xs, final_g)
    new_k = jnp.stack(new_ks, axis=1)
    new_v = jnp.stack(new_vs, axis=1)
    return (y_prompt, y_sample, new_k, new_v)
```

```python
import numpy as np
from contextlib import ExitStack
import concourse.bass as bass
import concourse.mybir as mybir
from concourse.bass_utils import run_bass_kernel_spmd

F32 = mybir.dt.float32
BF16 = mybir.dt.bfloat16
AF = mybir.ActivationFunctionType
ALU = mybir.AluOpType
AX = mybir.AxisListType

D = 2048
NCH = 16
TS = 4096
TP = 1024
T = 512
HD = 128
NH = 8
NKV = 2
INW = 8704
DFF = 8192
EPS = 1e-6
SCALE = HD ** -0.5
NS = 5
ENGS = ("pe", "act", "dve", "pool", "sp")
MRES = ["mods", "gm1", "gm2"] + [("mod", j) for j in range(96)]

WT = []
for i in range(34):
    WT.append(("w_in", 0, 16, i * 256, 256))
for i in range(8):
    WT.append(("w_ao", 0, 8, i * 256, 256))
for i in range(8):
    WT.append(("w_co", 0, 8, i * 256, 256))
for i in range(8):
    WT.append(("w_o", 0, 16, i * 256, 256))
for i in range(32):
    WT.append(("w_ff1", 0, 16, i * 256, 256))
for j in range(16):
    for hf in range(2):
        WT.append(("w_ff2", hf * 32, 32, j * 128, 128))
I_IN, I_AO, I_CO, I_WO, I_F1, I_F2 = 0, 34, 42, 50, 58, 90
NWT = len(WT)
GROUPS = [[4, 5], list(range(0, 4)) + list(range(6, 18)), list(range(34, 50)) + list(range(18, 34)),
          list(range(50, 90)), list(range(90, 122))]
GRP_OF = {sx: g for g, lst in enumerate(GROUPS) for sx in lst}


class Sched:
    def __init__(self, dry):
        self.dry = dry
        self.streams = {e: [] for e in ENGS}
        self.semcnt = {}
        self.seen = {e: {} for e in ENGS}
        self.lastw = {}
        self.readers = {}

    def op(self, eng, fn, reads=(), writes=(), dma=None):
        if self.dry:
            return
        own = "c_" + eng
        raw = {}
        oth = {}
        for r in reads:
            t = self.lastw.get(r)
            if t is not None and raw.get(t[0], 0) < t[1]:
                raw[t[0]] = t[1]
        for w in writes:
            t = self.lastw.get(w)
            if t is not None and oth.get(t[0], 0) < t[1]:
                oth[t[0]] = t[1]
            for s, v in self.readers.get(w, {}).items():
                if oth.get(s, 0) < v:
                    oth[s] = v
        deps = dict(raw)
        for s, v in oth.items():
            if deps.get(s, 0) < v:
                deps[s] = v
        if eng == "pe":
            deps.pop(own, None)
        waits = []
        seen = self.seen[eng]
        for s, v in deps.items():
            if seen.get(s, 0) < v:
                seen[s] = v
                waits.append((s, v))
        if dma is None:
            sem, inc = own, 1
        else:
            sem, inc = "d_" + dma, 16
        val = self.semcnt.get(sem, 0) + inc
        self.semcnt[sem] = val
        self.streams[eng].append((waits, fn, sem, inc))
        for r in reads:
            d = self.readers.setdefault(r, {})
            if d.get(sem, 0) < val:
                d[sem] = val
        for w in writes:
            self.lastw[w] = (sem, val)
            self.readers[w] = {}

    def finish(self):
        if self.dry:
            return
        waits = [(s, v) for s, v in self.semcnt.items()]
        self.streams["sp"].append((waits, None, None, 0))


def rrows(off, nbytes):
    return [("R", k) for k in range(off // 1024, (off + nbytes - 1) // 1024 + 1)]


class Prog:
    def __init__(self, nc, S, tens, wseq):
        self.nc = nc
        self.S = S
        self.t = tens
        self.wseq = wseq
        self.wrec = []
        self.wi = 0
        self.nb = 0
        self.pool = list(range(8))
        self.ptn = 0
        self.alt = 0
        self.peng = "pool"
        self.npb = 0

    def bank(self):
        b = self.pool[self.nb % len(self.pool)]
        self.nb += 1
        return b

    def R(self, off, nbytes, dt, pattern=None, **kw):
        Rt = self.t["R"]
        v = Rt[:, off // 2:(off + nbytes) // 2]
        if dt == F32:
            v = v.bitcast(F32)
        if pattern is not None:
            v = v.rearrange(pattern, **kw)
        return v, rrows(off, nbytes)

    def ring(self, slot):
        return self.t["ring"][:, slot, :]

    def wload(self, n):
        if n >= len(self.wseq):
            return
        sidx = self.wseq[n]
        slot = n % NS
        _, _, nkc, _, ncols = WT[sidx]
        nv = nkc * ncols
        dst = self.ring(slot)[:, 0:nv]
        src = self.t["wscr"][sidx][:, 0:nv]
        self.S.op("sp", lambda e: e.dma_start(out=dst, in_=src),
                  reads=[("wscr", sx) for sx in GROUPS[GRP_OF[sidx]]], writes=[("ring", slot)], dma="ring%d" % slot)

    def wreq(self, sidx):
        n = self.wi
        self.wi += 1
        if self.wseq is None:
            self.wrec.append(sidx)
        else:
            assert self.wseq[n] == sidx, (n, self.wseq[n], sidx)
        slot = n % NS
        _, kc0, nkc, col0, ncols = WT[sidx]
        v = self.ring(slot)[:, 0:nkc * ncols].rearrange("p (k n) -> p k n", k=nkc)
        return v, ("ring", slot), n

    def wrel(self, n):
        if self.wseq is not None:
            self.wload(n + NS)

    def cp(self, eng, out, in_):
        if eng == "act":
            return lambda e: e.activation(out=out, in_=in_, func=AF.Copy)
        return lambda e: e.tensor_copy(out, in_)

    def alt2(self, a="act", b="dve"):
        self.alt += 1
        return a if self.alt % 2 else b

    def mmgroup(self, out, pairs):
        n = len(pairs)

        def f(e):
            ins = None
            for i, (l, r) in enumerate(pairs):
                ins = e.matmul(out, l, r, start=(i == 0), stop=(i == n - 1))
            return ins
        return f

    def mods_chunks(self, js, stg_offs):
        S, t = self.S, self.t
        wm = t["d_w_mod"].rearrange("(k p) n -> p k n", p=128)
        mod, sc = t["mod"], t["silu"]
        for i, j in enumerate(js):
            k = i % len(stg_offs)
            stg, sres = self.R(stg_offs[k], 8192, F32, "p (k n) -> p k n", k=16)
            S.op("sp", lambda e, stg=stg, j=j: e.dma_start(out=stg, in_=wm[:, :, j * 128:(j + 1) * 128]),
                 writes=sres, dma="mstg%d" % (stg_offs[k] // 8192))
            b = self.bank()
            pairs = [(stg[:, kc, :], sc[:, kc, :]) for kc in range(16)]
            S.op("pe", self.mmgroup(t["ps"][:, b, 0:2], pairs), reads=sres + ["silu"], writes=[("ps", b)])
            S.op("dve", lambda e, b=b, j=j: e.tensor_scalar(out=mod[:, j, :], in0=t["ps"][:, b, 0:2],
                                                             scalar1=t["c_bmodT"][:, j:j + 1], scalar2=None,
                                                             op0=ALU.add),
                 reads=[("ps", b), "bmodT"], writes=[("mod", j)])

    def mods_all(self):
        S, t = self.S, self.t
        wm = t["d_w_mod"].rearrange("(k p) n -> p k n", p=128)
        mod, sc, ps = t["mod"], t["silu"], t["ps"]
        tm = t["rl"][0:2, 0, :]
        last = []
        for cg in range(24):
            k = cg % 2
            stg, sres = self.R(k * 32768, 32768, F32, "p (k n) -> p k n", k=16)
            S.op("sp", lambda e, stg=stg, cg=cg: e.dma_start(out=stg, in_=wm[:, :, cg * 512:(cg + 1) * 512]),
                 writes=sres, dma="mstg%d" % k)
            if cg >= 22:
                last += sres
            b = self.bank()
            S.op("pe", self.mmgroup(ps[0:2, b, :], [(sc[:, kc, :], stg[:, kc, :]) for kc in range(16)]),
                 reads=sres + ["silu"], writes=[("ps", b)])
            S.op("dve", lambda e, b=b: e.tensor_copy(tm, ps[0:2, b, :]), reads=[("ps", b)], writes=[("rl", 0)])
            b2 = self.bank()

            def f(e, b2=b2):
                ins = None
                for k4 in range(4):
                    ins = e.transpose(ps[:, b2, k4 * 2:(k4 + 1) * 2], tm[:, k4 * 128:(k4 + 1) * 128], t["c_ident"][0:2, 0:2])
                return ins
            S.op("pe", f, reads=[("rl", 0), "ident"], writes=[("ps", b2)])
            S.op("dve", lambda e, b2=b2, cg=cg: e.tensor_tensor(
                out=mod[:, cg * 4:(cg + 1) * 4, :], in0=ps[:, b2, 0:8].rearrange("p (j r) -> p j r", j=4),
                in1=t["c_bmodT"][:, cg * 4:(cg + 1) * 4].unsqueeze(2).broadcast_to([128, 4, 2]), op=ALU.add),
                reads=[("ps", b2), "bmodT"], writes=[("mod", j) for j in range(cg * 4, cg * 4 + 4)])
        return last

    def mods_gm(self, gm, sl, ng, grp):
        S, t = self.S, self.t
        g, mod = t[gm], t["mod"]
        S.op("dve", lambda e: e.tensor_scalar(out=g[:], in0=mod[:, sl:sl + 16, :], scalar1=1.0, scalar2=None, op0=ALU.add),
             reads=[("mod", j) for j in range(sl, sl + 16)], writes=[gm])
        S.op("dve", lambda e: e.tensor_tensor(out=g[:], in0=g[:], in1=t["c_" + ng][:].unsqueeze(2).broadcast_to([128, 16, 2]),
                                              op=ALU.mult), reads=[gm, ng], writes=[gm])

    def prologue(self):
        S, t, nc = self.S, self.t, self.nc
        for name in ("ident", "condT", "bmodT", "n1g", "n2g", "fg", "gq_bc", "gk_bc", "convwT"):
            dst = t["c_" + name]
            src = t["d_" + name]
            S.op("sp", lambda e, dst=dst, src=src: e.dma_start(out=dst[:], in_=src),
                 writes=[name], dma=name)
        S.op("dve", lambda e: e.memset(t["ones_b"][:], 1.0), writes=["ones_b"])
        S.op("dve", lambda e: e.memset(t["ones_mean"][:], 1.0 / D), writes=["ones_mean"])
        S.op("dve", lambda e: e.memset(t["ones_f"][:], 1.0), writes=["ones_f"])
        S.op("dve", lambda e: e.memset(t["eps"][:], EPS), writes=["eps"])
        S.op("dve", lambda e: e.tensor_copy(t["ident_b"][:], t["c_ident"][:]), reads=["ident"], writes=["ident_b"])

        def convert(g, extra_reads):
            for sidx in GROUPS[g]:
                name, kc0, nkc, col0, ncols = WT[sidx]
                src = t["d_" + name].rearrange("(k p) n -> p k n", p=128)[:, kc0:kc0 + nkc, col0:col0 + ncols]
                dst = t["wscr"][sidx][:, 0:nkc * ncols].rearrange("p (k n) -> p k n", k=nkc)
                S.op("pool", lambda e, dst=dst, src=src: e.dma_start(out=dst, in_=src),
                     reads=extra_reads, writes=[("wscr", sidx)], dma="cvg%d" % g)
        for g in range(len(GROUPS)):
            convert(g, [])
        sc = t["silu"]
        S.op("act", lambda e: e.activation(out=sc[:], in_=t["c_condT"][:], func=AF.Silu),
             reads=["condT"], writes=["silu"])
        last = self.mods_all()
        self.mods_gm("gm1", 16, "n1g", 1)
        self.mods_gm("gm2", 64, "n2g", 4)

        ckf, ckr = self.R(0, 2048, F32, "p (b n) -> p b n", b=2)
        cvf, cvr = self.R(2048, 2048, F32, "p (b n) -> p b n", b=2)
        ckb, ckbr = self.R(4096, 1024, BF16, "p (b n) -> p b n", b=2)
        S.op("sp", lambda e: e.dma_start(out=ckf, in_=t["d_ck"].rearrange("(b p) n -> p b n", p=128)),
             writes=ckr, dma="ckf")
        S.op("sp", lambda e: e.dma_start(out=cvf, in_=t["d_cv"].rearrange("(b p) n -> p b n", p=128)),
             writes=cvr, dma="cvf")
        S.op("dve", lambda e: e.tensor_copy(ckb, ckf), reads=ckr, writes=ckbr)
        S.op("act", self.cp("act", t["Vc"][:], cvf), reads=cvr, writes=["Vc"])
        b = self.bank()
        psb = t["ps"][:, b, :].bitcast(BF16)

        def ftr(e):
            ins = None
            for blk in range(2):
                for g in range(2):
                    ins = e.transpose(psb[:, (g * 2 + blk) * 128:(g * 2 + blk + 1) * 128],
                                      ckb[:, blk, g * 128:(g + 1) * 128], t["ident_b"][:])
            return ins
        S.op("pe", ftr, reads=ckbr + ["ident_b"], writes=[("ps", b)])
        S.op("dve", lambda e: e.tensor_copy(t["KTc"][:].rearrange("p g n -> p (g n)"), psb[:, 0:512]),
             reads=[("ps", b)], writes=["KTc"])
        sqk, sqr = self.R(8192, 2048, F32)
        S.op("dve", lambda e: e.tensor_tensor(out=sqk, in0=t["KTc"][:].rearrange("p g n -> p (g n)"),
                                              in1=t["KTc"][:].rearrange("p g n -> p (g n)"), op=ALU.mult),
             reads=["KTc"], writes=sqr)
        b2 = self.bank()
        S.op("pe", lambda e: e.matmul(t["ps"][:, b2, :], t["ones_f"][:], sqk, start=True, stop=True),
             reads=sqr + ["ones_f"], writes=[("ps", b2)])
        sm = t["small"]
        S.op("dve", lambda e: e.tensor_reduce(out=sm[:, 0:1], in_=t["ps"][:, b2, :], axis=AX.X, op=ALU.max),
             reads=[("ps", b2)], writes=["small"])
        S.op("dve", lambda e: e.tensor_reduce(out=sm[:, 1:2], in_=t["c_gq_bc"][:], axis=AX.X, op=ALU.max,
                                              apply_absolute_value=True), reads=["gq_bc"], writes=["small"])
        S.op("dve", lambda e: e.tensor_reduce(out=sm[:, 2:3], in_=t["c_gk_bc"][:], axis=AX.X, op=ALU.max,
                                              apply_absolute_value=True), reads=["gk_bc"], writes=["small"])
        S.op("dve", lambda e: e.scalar_tensor_tensor(out=sm[:, 3:4], in0=sm[:, 2:3], scalar=float(HD),
                                                     in1=sm[:, 2:3], op0=ALU.mult, op1=ALU.mult),
             reads=["small"], writes=["small"])
        S.op("dve", lambda e: e.tensor_tensor(out=sm[:, 4:5], in0=sm[:, 3:4], in1=sm[:, 0:1], op=ALU.max),
             reads=["small"], writes=["small"])
        S.op("dve", lambda e: e.scalar_tensor_tensor(out=sm[:, 5:6], in0=sm[:, 1:2], scalar=sm[:, 1:2],
                                                     in1=sm[:, 4:5], op0=ALU.mult, op1=ALU.mult),
             reads=["small"], writes=["small"])
        S.op("act", lambda e: e.activation(out=sm[:, 6:7], in_=sm[:, 5:6], func=AF.Sqrt),
             reads=["small"], writes=["small"])
        S.op("dve", lambda e: e.tensor_scalar(out=t["negM"][:], in0=sm[:, 6:7], scalar1=-1.02, scalar2=None,
                                              op0=ALU.mult), reads=["small"], writes=["negM"])
        if self.wseq is not None:
            for n in range(NS):
                self.wload(n)

    def load_x(self, src, t0):
        for blk in range(4):
            self.load_x_blk(src, t0, blk)

    def norm(self, gm, sh, cond, mode, have=None):
        S, t = self.S, self.t
        xT, ps, h = t["xT"], t["ps"], t["h"]
        xres = [("xT", c) for c in range(16)]
        sq, sqres = self.R(0, 32768, F32, "p (c n) -> p c n", c=16)
        ssum, ssres = self.R(32768 + 4096, 2048, F32)
        rstd, rsres = self.R(32768 + 6144, 2048, F32)
        if have != "rstd":
            if have == "acc":
                src_sum, src_res = t["rl"][:, 1, :], [("rl", 1)]
            else:
                S.op("act", lambda e: e.activation(out=sq, in_=xT[:], func=AF.Square), reads=xres, writes=sqres)
                S.op("dve", lambda e: e.tensor_reduce(out=ssum, in_=sq.rearrange("p c n -> p n c"), axis=AX.X, op=ALU.add),
                     reads=sqres, writes=ssres)
                src_sum, src_res = ssum, ssres
            b = self.bank()
            S.op("pe", lambda e: e.matmul(ps[:, b, :], t["ones_mean"][:], src_sum, start=True, stop=True),
                 reads=src_res + ["ones_mean"], writes=[("ps", b)])
            S.op("act", lambda e: e.activation(out=ssum, in_=ps[:, b, :], func=AF.Sqrt, bias=t["eps"][:, 0:1], scale=1.0),
                 reads=[("ps", b), "eps"], writes=ssres)
            S.op("dve", lambda e: e.reciprocal(rstd, ssum), reads=ssres, writes=rsres)
        for c in range(16):
            gsc = gm[:, c, cond:cond + 1]
            if mode == "h":
                tmp, tres = self.R(32768 + (c % 2) * 2048, 2048, F32)
                S.op("dve", lambda e, c=c, tmp=tmp, gsc=gsc: e.scalar_tensor_tensor(
                    out=tmp, in0=xT[:, c, :], scalar=gsc, in1=rstd, op0=ALU.mult, op1=ALU.mult),
                    reads=[("xT", c)] + rsres + MRES, writes=tres)
                S.op("act", lambda e, c=c, tmp=tmp: e.activation(
                    out=h[:, c, 0:512], in_=tmp, func=AF.Identity, bias=sh[:, c, cond:cond + 1], scale=1.0),
                    reads=tres + MRES, writes=[("h", c)])
            else:
                S.op("dve", lambda e, c=c, gsc=gsc: e.scalar_tensor_tensor(
                    out=sq[:, c, :], in0=xT[:, c, :], scalar=gsc, in1=rstd, op0=ALU.mult, op1=ALU.mult),
                    reads=[("xT", c)] + rsres + MRES, writes=rrows(c * 2048, 2048))

    def halo(self, src, t0, ntok, gm, sh, cond):
        S, t = self.S, self.t
        ps, h, xh = t["ps"], t["h"], t["xh"]
        tl = max(t0 - 1, 0)
        tr = min(t0 + T, ntok - 1)
        S.op("sp", lambda e: e.dma_start(out=xh[:, 0:128], in_=src[tl, :].rearrange("(c p) -> c p", p=128)),
             writes=["xh"], dma="xh")
        S.op("sp", lambda e: e.dma_start(out=xh[:, 128:256], in_=src[tr, :].rearrange("(c p) -> c p", p=128)),
             writes=["xh"], dma="xh")
        b = self.bank()

        def f(e):
            e.transpose(ps[:, b, 0:16], xh[:, 0:128], t["c_ident"][0:16, 0:16])
            return e.transpose(ps[:, b, 16:32], xh[:, 128:256], t["c_ident"][0:16, 0:16])
        S.op("pe", f, reads=["xh", "ident"], writes=[("ps", b)])
        hs = t["hsm"]
        xhT = hs[:, 0:32].rearrange("p (t c) -> p t c", t=2)
        sqh = hs[:, 32:64].rearrange("p (t c) -> p t c", t=2)
        ssh = hs[:, 64:66]
        rsh = hs[:, 66:68]
        th = hs[:, 96:128].rearrange("p (t c) -> p t c", t=2)
        S.op("dve", lambda e: e.tensor_copy(hs[:, 0:32], ps[:, b, 0:32]), reads=[("ps", b)], writes=["hsm"])
        S.op("dve", lambda e: e.tensor_tensor(out=sqh, in0=xhT, in1=xhT, op=ALU.mult), reads=["hsm"], writes=["hsm"])
        S.op("dve", lambda e: e.tensor_reduce(out=ssh, in_=sqh, axis=AX.X, op=ALU.add), reads=["hsm"], writes=["hsm"])
        b2 = self.bank()
        S.op("pe", lambda e: e.matmul(ps[:, b2, 0:2], t["ones_mean"][:], ssh, start=True, stop=True),
             reads=["hsm", "ones_mean"], writes=[("ps", b2)])
        S.op("act", lambda e: e.activation(out=ssh, in_=ps[:, b2, 0:2], func=AF.Sqrt, bias=t["eps"][:, 0:1], scale=1.0),
             reads=[("ps", b2), "eps"], writes=["hsm"])
        S.op("dve", lambda e: e.reciprocal(rsh, ssh), reads=["hsm"], writes=["hsm"])
        S.op("dve", lambda e: e.tensor_tensor(out=th, in0=xhT, in1=gm[:, :, cond].unsqueeze(1).broadcast_to([128, 2, 16]),
                                              op=ALU.mult), reads=["hsm"] + MRES, writes=["hsm"])
        S.op("dve", lambda e: e.tensor_tensor(out=th, in0=th, in1=rsh.unsqueeze(2).broadcast_to([128, 2, 16]),
                                              op=ALU.mult), reads=["hsm"], writes=["hsm"])
        S.op("dve", lambda e: e.tensor_tensor(out=t["hh"][:], in0=th.rearrange("p t c -> p c t"),
                                              in1=sh[:, :, cond:cond + 1].broadcast_to([128, 16, 2]), op=ALU.add),
             reads=["hsm"] + MRES, writes=["hhalo"])

    def load_rope(self, t0):
        S, t = self.S, self.t
        for nm in ("cos", "sin"):
            dst = t["rope_" + nm]
            src = t["d_rope_" + nm][t0:t0 + T, :].rearrange("(b p) n -> p b n", p=128)
            S.op("sp", lambda e, dst=dst, src=src: e.dma_start(out=dst[:], in_=src), writes=["rope_" + nm],
                 dma="rope_" + nm)

    def tokproj(self, sidx, dst_fn, dres_fn):
        S, t = self.S, self.t
        ps, h = t["ps"], t["h"]
        w, wres, wn = self.wreq(sidx)
        hres = [("h", c) for c in range(16)]
        for blk in range(4):
            b = self.bank()
            pairs = [(h[:, kc, blk * 128:(blk + 1) * 128], w[:, kc, :]) for kc in range(16)]
            S.op("pe", self.mmgroup(ps[:, b, 0:256], pairs), reads=[wres] + hres, writes=[("ps", b)])
            eng = self.alt2()
            S.op(eng, self.cp(eng, dst_fn(blk), ps[:, b, 0:256]), reads=[("ps", b)], writes=dres_fn(blk))
        self.wrel(wn)

    def headnorm_g(self, src4, sres, nh, gain_bc, gname, rope, dst_bf, dres, dst_f32=None, d32res=None):
        S, t = self.S, self.t
        G = 4 * nh
        n = G * 128
        S1, S1r = self.R(32768, n * 4, F32, "p (g d) -> p g d", g=G)
        sm = t["small2"]
        ssq = sm[:, 0:G]
        rs = sm[:, 32:32 + G]
        S.op("dve", lambda e: e.tensor_tensor(out=S1.rearrange("p (b h) d -> p b h d", b=4), in0=src4, in1=src4, op=ALU.mult),
             reads=sres, writes=S1r)
        yield
        S.op("dve", lambda e: e.tensor_reduce(out=ssq, in_=S1, axis=AX.X, op=ALU.add), reads=S1r, writes=["small2"])
        yield
        S.op("act", lambda e: e.activation(out=ssq, in_=ssq, func=AF.Sqrt, bias=t["eps"][:, 0:1], scale=1.0 / HD),
             reads=["small2", "eps"], writes=["small2"])
        yield
        S.op("dve", lambda e: e.reciprocal(rs, ssq), reads=["small2"], writes=["small2"])
        yield
        rsb = rs.rearrange("p (b h) -> p b h", b=4).unsqueeze(3).broadcast_to([128, 4, nh, 128])
        S.op("dve", lambda e: e.tensor_tensor(out=src4, in0=src4, in1=rsb, op=ALU.mult), reads=sres + ["small2"], writes=sres)
        yield
        gb = gain_bc[:].unsqueeze(1).unsqueeze(1).broadcast_to([128, 4, nh, 128])
        if not rope:
            if dst_f32 is not None:
                S.op(self.peng, lambda e: e.tensor_tensor(out=dst_f32, in0=src4, in1=gb, op=ALU.mult),
                     reads=sres + [gname], writes=d32res)
                yield
                S.op("act", self.cp("act", dst_bf, dst_f32), reads=d32res, writes=dres)
                yield
            else:
                S.op(self.peng, lambda e: e.tensor_tensor(out=dst_bf, in0=src4, in1=gb, op=ALU.mult),
                     reads=sres + [gname], writes=dres)
                yield
            return
        S.op(self.peng, lambda e: e.tensor_tensor(out=src4, in0=src4, in1=gb, op=ALU.mult), reads=sres + [gname], writes=sres)
        yield
        x0 = src4[:, :, :, 0::2]
        x1 = src4[:, :, :, 1::2]
        cs = t["rope_cos"][:].unsqueeze(2).broadcast_to([128, 4, nh, 64])
        sn = t["rope_sin"][:].unsqueeze(2).broadcast_to([128, 4, nh, 64])
        hb = G * 64 * 4
        T1, T1r = self.R(32768, hb, F32, "p (b h d) -> p b h d", b=4, h=nh)
        T2, T2r = self.R(32768 + hb, hb, F32, "p (b h d) -> p b h d", b=4, h=nh)
        pe_ = self.peng
        S.op("dve", lambda e: e.tensor_tensor(out=T1, in0=x0, in1=cs, op=ALU.mult), reads=sres + ["rope_cos"], writes=T1r)
        yield
        S.op(pe_, lambda e: e.tensor_tensor(out=T2, in0=x1, in1=sn, op=ALU.mult), reads=sres + ["rope_sin"], writes=T2r)
        yield
        S.op("dve", lambda e: e.tensor_tensor(out=dst_bf[:, :, :, 0::2], in0=T1, in1=T2, op=ALU.subtract),
             reads=T1r + T2r, writes=dres)
        yield
        S.op(pe_, lambda e: e.tensor_tensor(out=T1, in0=x0, in1=sn, op=ALU.mult), reads=sres + ["rope_sin"], writes=T1r)
        yield
        S.op("dve", lambda e: e.tensor_tensor(out=T2, in0=x1, in1=cs, op=ALU.mult), reads=sres + ["rope_cos"], writes=T2r)
        yield
        S.op(pe_, lambda e: e.tensor_tensor(out=dst_bf[:, :, :, 1::2], in0=T1, in1=T2, op=ALU.add),
             reads=T1r + T2r, writes=dres)
        yield

    def tposeT_g(self, src_bf, sres, nh, dst_fn, dres):
        S, t = self.S, self.t
        for blk in range(4):
            b = self.bank()
            psb = t["ps"][:, b, :].bitcast(BF16)

            def f(e, blk=blk, psb=psb):
                ins = None
                for hd in range(nh):
                    ins = e.transpose(psb[:, hd * 128:(hd + 1) * 128], src_bf[:, blk, hd, :], t["ident_b"][:])
                return ins
            S.op("pe", f, reads=sres + ["ident_b"], writes=[("ps", b)])
            yield
            eng = self.alt2()
            S.op(eng, self.cp(eng, dst_fn(blk), psb[:, 0:nh * 128].rearrange("p (h n) -> p h n", h=nh)),
                 reads=[("ps", b)], writes=dres)
            yield

    def headnorm(self, *a, **kw):
        for _ in self.headnorm_g(*a, **kw):
            pass

    def tposeT(self, *a, **kw):
        for _ in self.tposeT_g(*a, **kw):
            pass

    def kv_phase1(self, it, between=None):
        S, t = self.S, self.t
        t0 = it * T
        kvf, kvr = self.R(16384, 8192, F32, "p (b n) -> p b n", b=4)
        self.tokproj(I_IN + 4, lambda blk: kvf[:, blk, 0:256], lambda blk: kvr)
        self.tokproj(I_IN + 5, lambda blk: t["Vs"][:, it * 4 + blk, :], lambda blk: [("Vs", it * 4 + blk)])
        kr, krr = self.R(0, 2048, BF16, "p (b h d) -> p b h d", b=4, h=2)

        def chain():
            yield from self.headnorm_g(kvf[:, :, 0:256].rearrange("p b (h d) -> p b h d", h=2), kvr, 2, t["c_gk_bc"],
                                       "gk_bc", True, kr, krr)
            yield from self.tposeT_g(kr, krr, 2, lambda blk: t["KTs"][:, :, t0 + blk * 128:t0 + (blk + 1) * 128],
                                     [("KTs", it)])
        gen = chain()
        if between is not None:
            src, nt0 = between
            for blk in range(3):
                self.x_dma(src, nt0, blk, blk)
            for blk in range(4):
                self.load_x_blk(src, nt0, blk, k=(0, 1, 2, 0)[blk], do_dma=False)
                if blk == 0:
                    self.x_dma(src, nt0, 3, 0)
                for _ in range(4):
                    next(gen, None)
        for _ in gen:
            pass

    def q_proj(self, sample):
        qf, qfr = self.R(0, 16384, F32, "p (b n) -> p b n", b=4)
        for i in range(4):
            self.tokproj(I_IN + i, lambda blk, i=i: qf[:, blk, i * 256:(i + 1) * 256], lambda blk: qfr)
        if not sample:
            kvf, kvr = self.R(16384, 8192, F32, "p (b n) -> p b n", b=4)
            self.tokproj(I_IN + 4, lambda blk: kvf[:, blk, 0:256], lambda blk: kvr)
            self.tokproj(I_IN + 5, lambda blk: kvf[:, blk, 256:512], lambda blk: kvr)

    def q_stage_g(self, sample, t0, it):
        S, t = self.S, self.t
        qf, qfr = self.R(0, 16384, F32, "p (b h d) -> p b h d", b=4, h=8)
        qT, qTr = self.R(24576, 8192, BF16, "p (h n) -> p h n", h=8)
        qr, qrr = self.R(49152, 8192, BF16, "p (b h d) -> p b h d", b=4, h=8)
        yield from self.headnorm_g(qf, qfr, 8, t["c_gq_bc"], "gq_bc", sample, qr, qrr)
        yield from self.tposeT_g(qr, qrr, 8, lambda blk: qT[:, :, blk * 128:(blk + 1) * 128], qTr)
        if not sample:
            kvf, kvr = self.R(16384, 8192, F32, "p (b n) -> p b n", b=4)
            kout, koutr = self.R(57344, 4096, F32, "p (b h d) -> p b h d", b=4, h=2)
            KTp, KTpr = self.R(61440, 2048, BF16, "p (g n) -> p g n", g=2)
            Vp, Vpr = self.R(63488, 2048, BF16, "p (b n) -> p b n", b=4)
            kr, krr = self.R(0, 2048, BF16, "p (b h d) -> p b h d", b=4, h=2)
            yield from self.headnorm_g(kvf[:, :, 0:256].rearrange("p b (h d) -> p b h d", h=2), kvr, 2, t["c_gk_bc"], "gk_bc", False,
                          kr, krr, dst_f32=kout, d32res=koutr)
            yield from self.tposeT_g(kr, krr, 2, lambda blk: KTp[:, :, blk * 128:(blk + 1) * 128], KTpr)
            S.op("dve", lambda e: e.tensor_copy(Vp, kvf[:, :, 256:512]), reads=kvr, writes=Vpr)
            S.op("sp", lambda e: e.dma_start(out=t["o_nk"][t0:t0 + T, :].rearrange("(b p) n -> p b n", p=128),
                                             in_=kout.rearrange("p b h d -> p b (h d)")),
                 reads=koutr, writes=[("nk", t0)], dma="kout")
            S.op("sp", lambda e: e.dma_start(out=t["o_nv"][t0:t0 + T, :].rearrange("(b p) n -> p b n", p=128),
                                             in_=kvf[:, :, 256:512]),
                 reads=kvr, writes=[("nv", t0)], dma="vout")

    def pairbank(self):
        p = (4, 6)[self.npb % 2]
        self.npb += 1
        return p

    def attn_head(self, sample, hd, c0, n, nkc, bo, qT, qTr, aT, pk):
        S, t = self.S, self.t
        ps = t["ps"]
        g = hd // 4
        npair = nkc // 2
        bs = 2 + bo
        if pk is not None:
            KTp, KTpr, Vp, Vpr = pk

        def kv(kc):
            if sample:
                if kc < 32:
                    return (t["KTs"][:, g, kc * 128:(kc + 1) * 128], [("KTs", kc // 4)],
                            t["Vs"][:, kc, g * 128:(g + 1) * 128], [("Vs", kc)])
                return (t["KTc"][:, g, (kc - 32) * 128:(kc - 31) * 128], ["KTc"],
                        t["Vc"][:, kc - 32, g * 128:(g + 1) * 128], ["Vc"])
            blk = c0 // 128 + kc
            return (KTp[:, g, blk * 128:(blk + 1) * 128], KTpr, Vp[:, blk, g * 128:(g + 1) * 128], Vpr)
        acc, accr = self.R(40960, 2048, F32)

        def s_op(kp):
            kT0, kr0, _, _ = kv(2 * kp)
            kT1, kr1, _, _ = kv(2 * kp + 1)
            b = self.pairbank()

            def f(e):
                e.matmul(ps[:, b, 0:n], kT0, qT[:, hd, c0:c0 + n], start=True, stop=True)
                return e.matmul(ps[:, b + 1, 0:n], kT1, qT[:, hd, c0:c0 + n], start=True, stop=True)
            S.op("pe", f, reads=list(kr0) + list(kr1) + qTr, writes=[("ps", b), ("ps", b + 1)])
            return b

        def pv_op(kp, b):
            _, _, v0, vr0 = kv(2 * kp)
            _, _, v1, vr1 = kv(2 * kp + 1)
            k = self.ptn % 4
            self.ptn += 1
            pt, ptr = self.R(32768 + k * 2048, 2048, BF16, "p (a n) -> p a n", a=2)
            S.op("act", lambda e: e.activation(out=pt[:, :, 0:n], in_=ps[:, b:b + 2, 0:n], func=AF.Exp,
                                               bias=t["negM"][:, 0:1], scale=SCALE),
                 reads=[("ps", b), ("ps", b + 1), "negM"], writes=ptr)

            def f(e):
                e.matmul(ps[:, bo, 0:n], v0, pt[:, 0, 0:n], start=(kp == 0), stop=False)
                e.matmul(ps[:, bo, 0:n], v1, pt[:, 1, 0:n], start=False, stop=(kp == npair - 1))
                return e.matmul(ps[:, bs, 0:n], t["ones_b"][:], pt[:, 1, 0:n], start=(kp == 0), stop=False)
            S.op("pe", f, reads=ptr + list(vr0) + list(vr1) + ["ones_b"], writes=[("ps", bo), ("ps", bs)])
            if kp == 0:
                S.op("dve", lambda e: e.tensor_copy(acc[:, 0:n], pt[:, 0, 0:n]), reads=ptr, writes=accr)
            else:
                S.op("dve", lambda e: e.tensor_tensor(out=acc[:, 0:n], in0=acc[:, 0:n], in1=pt[:, 0, 0:n], op=ALU.add),
                     reads=ptr + accr, writes=accr)

        LA = 2
        banks = {}
        for kp in range(min(LA, npair)):
            banks[kp] = s_op(kp)
        for kp in range(npair):
            pv_op(kp, banks.pop(kp))
            if kp + LA < npair:
                banks[kp + LA] = s_op(kp + LA)
        S.op("pe", lambda e: e.matmul(ps[:, bs, 0:n], t["ones_f"][:], acc[:, 0:n], start=False, stop=True),
             reads=accr + ["ones_f"], writes=[("ps", bs)])
        rs, rsr = self.R(49152, 2048, F32)
        S.op("dve", lambda e: e.reciprocal(rs[:, 0:n], ps[:, bs, 0:n]), reads=[("ps", bs)], writes=rsr)
        S.op("dve", lambda e: e.tensor_tensor(out=aT[:, hd, c0:c0 + n], in0=ps[:, bo, 0:n], in1=rs[:, 0:n], op=ALU.mult),
             reads=[("ps", bo)] + rsr, writes=rrows(16384 + hd * 1024, 1024))

    def attention(self, sample):
        S, t = self.S, self.t
        qT, qTr = self.R(24576, 8192, BF16, "p (h n) -> p h n", h=8)
        aT, aTr = self.R(16384, 8192, BF16, "p (h n) -> p h n", h=8)
        if sample:
            segs = [(0, T, 34)]
            pk = None
        else:
            segs = [(0, 256, 2), (256, 256, 2)]
            KTp, KTpr = self.R(61440, 2048, BF16, "p (g n) -> p g n", g=2)
            Vp, Vpr = self.R(63488, 2048, BF16, "p (b n) -> p b n", b=4)
            pk = (KTp, KTpr, Vp, Vpr)
        nacc = 0
        for (c0, n, nkc) in segs:
            for hd in range(NH):
                bo = nacc % 2
                nacc += 1
                self.attn_head(sample, hd, c0, n, nkc, bo, qT, qTr, aT, pk)

    def conv(self, sample, first, last, cond, offs=(32768, 36864, 38912, 24576), side=None, nside=3):
        S, t = self.S, self.t
        ps, h = t["ps"], t["h"]
        hres = [("h", c) for c in range(16)]
        o_cu, o_uf, o_tc, o_cT = offs
        cT, cTr = self.R(o_cT, 8192, BF16, "p (j n) -> p j n", j=8)
        cu, cur = self.R(o_cu, 516 * 4, F32)
        uf, ufr = self.R(o_uf, 2048, F32)
        tc_, tcr = self.R(o_tc, 2048, F32)
        uh = t["hsm"][:, 128:130]
        if sample:
            segs = [(0, T, 1)]
        else:
            segs = [(0, 256, 1), (256, 256, 259)]
            S.op("dve", lambda e: e.memset(cu, 0.0), writes=cur)
        cw = t["c_convwT"]
        for jp in range(4):
            wu, wur, nu = self.wreq(I_IN + 6 + jp)
            wc, wcr, ncc = self.wreq(I_IN + 10 + jp)
            wb, wbr, nbb = self.wreq(I_IN + 14 + jp)
            for jj in range(2):
                j = jp * 2 + jj
                sl = slice(jj * 128, (jj + 1) * 128)
                bu = self.bank()
                S.op("pe", self.mmgroup(ps[:, bu, :], [(wu[:, kc, sl], h[:, kc, 0:512]) for kc in range(16)]),
                     reads=[wur] + hres, writes=[("ps", bu)])
                S.op("act", self.cp("act", uf, ps[:, bu, :]), reads=[("ps", bu)], writes=ufr)
                if sample:
                    bh = self.bank()

                    def fh(e, bh=bh, sl=sl, wu=wu, wc=wc):
                        for kc in range(16):
                            e.matmul(ps[:, bh, 0:2], wu[:, kc, sl], t["hh"][:, kc, :], start=(kc == 0), stop=(kc == 15))
                        ins = None
                        for kc in range(16):
                            ins = e.matmul(ps[:, bh, 2:4], wc[:, kc, sl], t["hh"][:, kc, :], start=(kc == 0), stop=(kc == 15))
                        return ins
                    S.op("pe", fh, reads=[wur, wcr, "hhalo"], writes=[("ps", bh)])
                    S.op("act", self.cp("act", uh, ps[:, bh, 0:2]), reads=[("ps", bh)], writes=["uh"])
                bc = self.bank()
                S.op("pe", self.mmgroup(ps[:, bc, :], [(wc[:, kc, sl], h[:, kc, 0:512]) for kc in range(16)]),
                     reads=[wcr] + hres, writes=[("ps", bc)])
                for (x0, n, c0) in segs:
                    S.op("dve", lambda e, x0=x0, n=n, c0=c0, bc=bc: e.tensor_tensor(
                        out=cu[:, c0:c0 + n], in0=uf[:, x0:x0 + n], in1=ps[:, bc, x0:x0 + n], op=ALU.mult),
                        reads=ufr + [("ps", bc)], writes=cur)
                if sample:
                    S.op("dve", lambda e, bh=bh: e.tensor_tensor(out=cu[:, 0:514:513], in0=uh, in1=ps[:, bh, 2:4], op=ALU.mult),
                         reads=["uh", ("ps", bh)], writes=cur)
                    if first:
                        S.op("dve", lambda e: e.memset(cu[:, 0:1], 0.0), writes=cur)
                    if last:
                        S.op("dve", lambda e: e.memset(cu[:, 513:514], 0.0), writes=cur)
                for (x0, n, c0) in segs:
                    a = c0 - 1
                    S.op("dve", lambda e, x0=x0, n=n, a=a, j=j: e.tensor_scalar(
                        out=tc_[:, x0:x0 + n], in0=cu[:, a:a + n], scalar1=cw[:, j, 0:1], scalar2=None, op0=ALU.mult),
                        reads=cur + ["convwT"], writes=tcr)
                    for k in (1, 2):
                        S.op("dve", lambda e, x0=x0, n=n, a=a, j=j, k=k: e.scalar_tensor_tensor(
                            out=tc_[:, x0:x0 + n], in0=cu[:, a + k:a + k + n], scalar=cw[:, j, k:k + 1],
                            in1=tc_[:, x0:x0 + n], op0=ALU.mult, op1=ALU.add),
                            reads=cur + tcr + ["convwT"], writes=tcr)
                bb = self.bank()
                S.op("pe", self.mmgroup(ps[:, bb, :], [(wb[:, kc, sl], h[:, kc, 0:512]) for kc in range(16)]),
                     reads=[wbr] + hres, writes=[("ps", bb)])
                S.op("dve", lambda e, j=j, bb=bb: e.tensor_tensor(out=cT[:, j, :], in0=tc_, in1=ps[:, bb, :], op=ALU.mult),
                     reads=tcr + [("ps", bb)], writes=rrows(o_cT + j * 1024, 1024))
                if side is not None:
                    for _ in range(nside):
                        next(side, None)
            self.wrel(nu)
            self.wrel(ncc)
            self.wrel(nbb)
        if side is not None:
            for _ in side:
                pass

    def merge(self, o_cT=24576):
        S, t = self.S, self.t
        ps, h = t["ps"], t["h"]
        hres = [("h", c) for c in range(16)]
        aT, aTr = self.R(16384, 8192, BF16, "p (h n) -> p h n", h=8)
        cT, cTr = self.R(o_cT, 8192, BF16, "p (j n) -> p j n", j=8)
        mg, mgr = self.R(0, 16384, BF16, "p (j n) -> p j n", j=16)
        for jp in range(8):
            wa_, war, na = self.wreq(I_AO + jp)
            wc_, wcr, ncx = self.wreq(I_CO + jp)
            wga, wgar, nga = self.wreq(I_IN + 18 + jp)
            wgs, wgsr, ngs = self.wreq(I_IN + 26 + jp)
            for jj in range(2):
                j = jp * 2 + jj
                sl = slice(jj * 128, (jj + 1) * 128)
                bA, bC, ba, bs = self.bank(), self.bank(), self.bank(), self.bank()
                S.op("pe", self.mmgroup(ps[:, bA, :], [(wa_[:, kc, sl], aT[:, kc, :]) for kc in range(8)]),
                     reads=[war] + aTr, writes=[("ps", bA)])
                S.op("pe", self.mmgroup(ps[:, bC, :], [(wc_[:, kc, sl], cT[:, kc, :]) for kc in range(8)]),
                     reads=[wcr] + cTr, writes=[("ps", bC)])
                S.op("pe", self.mmgroup(ps[:, ba, :], [(wga[:, kc, sl], h[:, kc, 0:512]) for kc in range(16)]),
                     reads=[wgar] + hres, writes=[("ps", ba)])
                S.op("pe", self.mmgroup(ps[:, bs, :], [(wgs[:, kc, sl], h[:, kc, 0:512]) for kc in range(16)]),
                     reads=[wgsr] + hres, writes=[("ps", bs)])
                o = 32768 + jj * 8192
                sa, sar = self.R(o, 2048, F32)
                ss, ssr = self.R(o + 2048, 2048, F32)
                t1, t1r = self.R(o + 4096, 2048, F32)
                t2, t2r = self.R(o + 6144, 2048, F32)
                S.op("act", lambda e, sa=sa, ba=ba: e.activation(out=sa, in_=ps[:, ba, :], func=AF.Sigmoid),
                     reads=[("ps", ba)], writes=sar)
                S.op("act", lambda e, ss=ss, bs=bs: e.activation(out=ss, in_=ps[:, bs, :], func=AF.Sigmoid),
                     reads=[("ps", bs)], writes=ssr)
                S.op("dve", lambda e, t1=t1, sa=sa, bA=bA: e.tensor_tensor(out=t1, in0=sa, in1=ps[:, bA, :], op=ALU.mult),
                     reads=sar + [("ps", bA)], writes=t1r)
                S.op("dve", lambda e, t2=t2, ss=ss, bC=bC: e.tensor_tensor(out=t2, in0=ss, in1=ps[:, bC, :], op=ALU.mult),
                     reads=ssr + [("ps", bC)], writes=t2r)
                S.op("pool", lambda e, t1=t1, t2=t2, j=j: e.tensor_tensor(out=mg[:, j, :], in0=t1, in1=t2, op=ALU.add),
                     reads=t1r + t2r, writes=rrows(j * 1024, 1024))
            for n_ in (na, ncx, nga, ngs):
                self.wrel(n_)

    def sq_acc(self, j):
        S, t = self.S, self.t
        rl, xT = t["rl"], t["xT"]
        if j == 0:
            S.op("act", lambda e: e.activation(out=rl[:, 1, :], in_=xT[:, 0, :], func=AF.Square),
                 reads=[("xT", 0)], writes=[("rl", 1)])
            return
        S.op("act", lambda e: e.activation(out=rl[:, 0, :], in_=xT[:, j, :], func=AF.Square),
             reads=[("xT", j)], writes=[("rl", 0)])
        S.op("dve", lambda e: e.tensor_tensor(out=rl[:, 1, :], in0=rl[:, 1, :], in1=rl[:, 0, :], op=ALU.add),
             reads=[("rl", 0), ("rl", 1)], writes=[("rl", 1)])

    def resid_proj(self, base, ntiles_per_chunk, nkc, rhs_fn, rres, gate, cond):
        S, t = self.S, self.t
        ps, xT = t["ps"], t["xT"]
        if ntiles_per_chunk == 0:
            for jp in range(8):
                w, wr, wn = self.wreq(base + jp)
                for jj in range(2):
                    j = jp * 2 + jj
                    b = self.bank()
                    S.op("pe", self.mmgroup(ps[:, b, :], [(w[:, kc, jj * 128:(jj + 1) * 128], rhs_fn(kc)) for kc in range(nkc)]),
                         reads=[wr] + rres, writes=[("ps", b)])
                    S.op("dve", lambda e, j=j, b=b: e.scalar_tensor_tensor(
                        out=xT[:, j, :], in0=ps[:, b, :], scalar=gate[:, j, cond:cond + 1], in1=xT[:, j, :],
                        op0=ALU.mult, op1=ALU.add), reads=[("ps", b), ("xT", j)] + MRES, writes=[("xT", j)])
                    self.sq_acc(j)
                self.wrel(wn)
        else:
            for j in range(16):
                w0, w0r, n0 = self.wreq(base + 2 * j)
                w1, w1r, n1 = self.wreq(base + 2 * j + 1)
                b = self.bank()
                pairs = [(w0[:, kc, :], rhs_fn(kc)) for kc in range(32)] + [(w1[:, kc, :], rhs_fn(32 + kc)) for kc in range(32)]
                S.op("pe", self.mmgroup(ps[:, b, :], pairs), reads=[w0r, w1r] + rres, writes=[("ps", b)])
                S.op("dve", lambda e, j=j, b=b: e.scalar_tensor_tensor(
                    out=xT[:, j, :], in0=ps[:, b, :], scalar=gate[:, j, cond:cond + 1], in1=xT[:, j, :],
                    op0=ALU.mult, op1=ALU.add), reads=[("ps", b), ("xT", j)] + MRES, writes=[("xT", j)])
                self.sq_acc(j)
                self.wrel(n0)
                self.wrel(n1)

    def ff1(self):
        S, t = self.S, self.t
        ps, h = t["ps"], t["h"]
        hres = [("h", c) for c in range(16)]
        hid, _ = self.R(0, 65536, BF16, "p (c n) -> p c n", c=64)
        rl = t["rl"]
        for i in range(32):
            w, wr, wn = self.wreq(I_F1 + i)
            for jj in range(2):
                c = i * 2 + jj
                b = self.bank()
                S.op("pe", self.mmgroup(ps[:, b, :], [(w[:, kc, jj * 128:(jj + 1) * 128], h[:, kc, 0:512]) for kc in range(16)]),
                     reads=[wr] + hres, writes=[("ps", b)])
                k = c % 2
                S.op("act", lambda e, k=k, b=b: e.activation(out=rl[:, k, :], in_=ps[:, b, :], func=AF.Relu),
                     reads=[("ps", b)], writes=[("rl", k)])
                eng = "pool" if c % 2 else "dve"
                S.op(eng, lambda e, k=k, c=c: e.tensor_tensor(out=hid[:, c, :], in0=rl[:, k, :], in1=rl[:, k, :], op=ALU.mult),
                     reads=[("rl", k)], writes=[("R", c)])
            self.wrel(wn)

    def xstg(self, k):
        t = self.t
        if k == 0:
            return t["xstage"][:], ["xstage"], "xstage"
        hf = t["h"][:].rearrange("p c n -> p (c n)")[:, (k - 1) * 4096:k * 4096].bitcast(F32)
        return hf, [("h", c) for c in range(16)], "hstg%d" % k

    def x_dma(self, src, t0, blk, k, eng="sp"):
        xs, xres, sem = self.xstg(k)
        self.S.op(eng, lambda e: e.dma_start(out=xs, in_=src[t0 + blk * 128:t0 + (blk + 1) * 128, :]),
                  writes=xres, dma=sem)

    def load_x_blk(self, src, t0, blk, k=0, do_dma=True):
        S, t = self.S, self.t
        xT, ps = t["xT"], t["ps"]
        xs, xres, _ = self.xstg(k)
        if do_dma:
            self.x_dma(src, t0, blk, k)
        st = t["xstat"]
        junk = t["rl"][:].rearrange("p a n -> p (a n)").bitcast(BF16)
        S.op("act", lambda e: e.activation(out=junk, in_=xs, func=AF.Square, accum_out=st[:, blk:blk + 1]),
             reads=xres, writes=[("rl", 0), ("rl", 1), ("xstat", blk)])
        S.op("act", lambda e: e.activation(out=st[:, 4 + blk:5 + blk], in_=st[:, blk:blk + 1], func=AF.Sqrt,
                                           bias=t["eps"][:, 0:1], scale=1.0 / D),
             reads=[("xstat", blk), "eps"], writes=[("xstat", 4 + blk)])
        S.op("dve", lambda e: e.reciprocal(st[:, 8 + blk:9 + blk], st[:, 4 + blk:5 + blk]),
             reads=[("xstat", 4 + blk)], writes=[("xstat", 8 + blk)])
        dg = t["diag"]
        S.op("dve", lambda e: e.tensor_scalar(out=dg[:], in0=t["c_ident"][:], scalar1=st[:, 8 + blk:9 + blk], scalar2=None,
                                              op0=ALU.mult), reads=[("xstat", 8 + blk), "ident"], writes=["diag"])
        for cg in range(4):
            b = self.bank()

            def f(e, b=b, cg=cg):
                ins = None
                for k in range(4):
                    c = cg * 4 + k
                    ins = e.transpose(ps[:, b, k * 128:(k + 1) * 128], xs[:, c * 128:(c + 1) * 128], t["c_ident"][:])
                return ins
            S.op("pe", f, reads=xres + ["ident"], writes=[("ps", b)])
            eng = self.alt2()
            out = xT[:, cg * 4:(cg + 1) * 4, blk * 128:(blk + 1) * 128]
            in_ = ps[:, b, :].rearrange("p (k n) -> p k n", k=4)
            S.op(eng, self.cp(eng, out, in_), reads=[("ps", b)], writes=[("xT", c) for c in range(cg * 4, cg * 4 + 4)])
        bd = self.bank()
        S.op("pe", lambda e: e.matmul(ps[:, bd, 0:128], t["ones_f"][:], dg[:], start=True, stop=True),
             reads=["diag", "ones_f"], writes=[("ps", bd)])
        rstd, rsres = self.R(32768 + 6144, 2048, F32)
        S.op("dve", lambda e: e.tensor_copy(rstd[:, blk * 128:(blk + 1) * 128], ps[:, bd, 0:128]),
             reads=[("ps", bd)], writes=rsres)

    def store_y(self, dst, t0, nxt=None):
        S, t = self.S, self.t
        ps = t["ps"]
        yT, _ = self.R(0, 32768, F32, "p (c n) -> p c n", c=16)
        for blk in range(4):
            ys, ysr = self.R(40960 + (blk % 2) * 8192, 8192, F32)
            for cg in range(4):
                b = self.bank()

                def f(e, b=b, cg=cg, blk=blk):
                    ins = None
                    for k in range(4):
                        c = cg * 4 + k
                        ins = e.transpose(ps[:, b, k * 128:(k + 1) * 128], yT[:, c, blk * 128:(blk + 1) * 128], t["c_ident"][:])
                    return ins
                S.op("pe", f, reads=sum([rrows(c * 2048, 2048) for c in range(cg * 4, cg * 4 + 4)], []) + ["ident"],
                     writes=[("ps", b)])
                eng = self.alt2()
                S.op(eng, self.cp(eng, ys[:, cg * 512:(cg + 1) * 512], ps[:, b, :]), reads=[("ps", b)], writes=ysr)
            S.op("sp", lambda e, blk=blk, ys=ys: e.dma_start(out=dst[t0 + blk * 128:t0 + (blk + 1) * 128, :], in_=ys),
                 reads=ysr, writes=[("y", id(dst), t0, blk)], dma="yst%d" % (blk % 2))
            if nxt is not None:
                self.load_x_blk(nxt[0], nxt[1], blk, k=(0, 1, 2, 0)[blk], do_dma=False)
                if blk == 0:
                    self.x_dma(nxt[0], nxt[1], 3, 0)

    def tile_phase2(self, sample, it, preloaded, nxt, halo_done=False, nxt_halo=None):
        t = self.t
        cond = 0 if sample else 1
        src = t["d_xs"] if sample else t["d_xp"]
        dst = t["o_ys"] if sample else t["o_yp"]
        ntok = TS if sample else TP
        t0 = it * T
        mod = t["mod"]
        sh1, g1 = mod[:, 0:16, :], mod[:, 32:48, :]
        sh1_lat = sh1
        sh2, g2 = mod[:, 48:64, :], mod[:, 80:96, :]
        if not preloaded:
            self.load_x(src, t0)
        if sample:
            self.load_rope(t0)
        self.norm(t["gm1"], sh1, cond, "h", have="rstd")
        if sample and not halo_done:
            self.halo(src, t0, ntok, t["gm1"], sh1, cond)
        self.q_proj(sample)
        if sample:
            self.conv(sample, it == 0, it == TS // T - 1, cond, offs=(16384, 19456, 21504, 57344),
                      side=self.q_stage_g(sample, t0, it))
            self.attention(sample)
            self.merge(57344)
        else:
            for _ in self.q_stage_g(sample, t0, it):
                pass
            self.attention(sample)
            self.conv(sample, it == 0, it == TS // T - 1, cond)
            self.merge()
        mg, mgr = self.R(0, 16384, BF16, "p (j n) -> p j n", j=16)
        self.resid_proj(I_WO, 0, 16, lambda kc: mg[:, kc, :], mgr, g1, cond)
        self.norm(t["gm2"], sh2, cond, "h", have="acc")
        self.ff1()
        hid, hidr = self.R(0, 65536, BF16, "p (c n) -> p c n", c=64)
        if nxt is not None:
            for blk in range(3):
                self.x_dma(nxt[0], nxt[1], blk, blk, eng="act")
        if nxt_halo is not None:
            self.halo(t["d_xs"], nxt_halo, TS, t["gm1"], sh1_lat, 0)
        self.resid_proj(I_F2, 2, 64, lambda kc: hid[:, kc, :], hidr, g2, cond)
        self.norm(t["fgm"], None, 0, "y", have="acc")
        self.store_y(dst, t0, nxt)

    def run(self):
        t = self.t
        self.prologue()
        mod = t["mod"]
        self.S.op("dve", lambda e: e.tensor_copy(t["fgm"][:, :, 0], t["c_fg"][:]), reads=["fg"], writes=["mods"])
        self.peng = "dve"
        nt = TS // T
        self.load_x(t["d_xs"], 0)
        for it in range(nt):
            self.load_rope(it * T)
            self.norm(t["gm1"], mod[:, 0:16, :], 0, "h", have="rstd")

            between = (t["d_xs"], (it + 1) * T) if it + 1 < nt else (t["d_xp"], 0)
            self.kv_phase1(it, between)
        self.peng = "pool"
        tiles = [(False, i) for i in range(TP // T)] + [(True, i) for i in range(TS // T)]
        for i, (sample, it) in enumerate(tiles):
            nxt = None
            if i + 1 < len(tiles):
                ns, ni = tiles[i + 1]
                nxt = (t["d_xs"] if ns else t["d_xp"], ni * T)
            nh_ = ni * T if (nxt is not None and ns) else None
            self.tile_phase2(sample, it, True, nxt, halo_done=(i > 0 and sample), nxt_halo=nh_)
        self.S.finish()


def declare(nc, es):
    t = {}

    def din(name, shape, dt=F32):
        t["d_" + name] = nc.dram_tensor(name, list(shape), dt, kind="ExternalInput").ap()

    def dout(name, shape):
        t["o_" + name] = nc.dram_tensor(name, list(shape), F32, kind="ExternalOutput").ap()
    din("xs", [TS, D]); din("xp", [TP, D]); din("ck", [256, 256]); din("cv", [256, 256])
    din("condT", [128, 16, 2]); din("w_mod", [D, 6 * D]); din("bmodT", [128, 96])
    din("n1g", [128, 16]); din("n2g", [128, 16]); din("fg", [128, 16])
    din("w_in", [D, INW]); din("gq_bc", [128, 128]); din("gk_bc", [128, 128]); din("convwT", [128, 8, 3])
    din("w_ao", [1024, D]); din("w_co", [1024, D]); din("w_o", [D, D]); din("w_ff1", [D, DFF]); din("w_ff2", [DFF, D])
    din("ident", [128, 128]); din("rope_cos", [TS, 64]); din("rope_sin", [TS, 64])
    dout("ys", [TS, D]); dout("yp", [TP, D]); dout("nk", [TP, 256]); dout("nv", [TP, 256])
    t["wscr"] = nc.dram_tensor("wscr", [NWT, 128, 4096], BF16, kind="Internal").ap()

    def sb(name, shape, dt):
        t[name] = es.enter_context(nc.sbuf_tensor("s_" + name, list(shape), dt))
    sb("R", [128, 32768], BF16)
    sb("KTs", [128, 2, TS], BF16); sb("KTc", [128, 2, 256], BF16)
    sb("Vs", [128, 32, 256], BF16); sb("Vc", [128, 2, 256], BF16)
    sb("xT", [128, 16, T], F32); sb("h", [128, 16, 514], BF16)
    sb("ring", [128, NS, 4096], BF16); sb("xstage", [128, D], F32)
    sb("c_ident", [128, 128], F32); sb("ident_b", [128, 128], BF16); sb("ones_b", [128, 128], BF16)
    sb("ones_mean", [128, 128], F32); sb("ones_f", [128, 128], F32)
    sb("c_condT", [128, 16, 2], F32); sb("silu", [128, 16, 2], F32); sb("c_bmodT", [128, 96], F32)
    sb("mod", [128, 96, 2], F32); sb("gm1", [128, 16, 2], F32); sb("gm2", [128, 16, 2], F32); sb("fgm", [128, 16, 1], F32)
    sb("c_n1g", [128, 16], F32); sb("c_n2g", [128, 16], F32); sb("c_fg", [128, 16], F32)
    sb("c_gq_bc", [128, 128], F32); sb("c_gk_bc", [128, 128], F32); sb("c_convwT", [128, 8, 3], F32)
    sb("rope_cos", [128, 4, 64], F32); sb("rope_sin", [128, 4, 64], F32)
    sb("negM", [128, 1], F32); sb("eps", [128, 1], F32); sb("small", [128, 8], F32); sb("small2", [128, 64], F32)
    sb("xstat", [128, 12], F32); sb("diag", [128, 128], F32); sb("hh", [128, 16, 2], BF16); sb("hsm", [128, 160], F32); sb("xh", [16, 256], F32); sb("rl", [128, 2, T], F32)
    t["ps"] = es.enter_context(nc.psum_tensor("ps", [128, 8, 512], F32))
    return t


def build():
    S0 = Sched(dry=True)
    P0 = Prog(None, S0, _DryTens(), None)
    P0.run()
    wseq = P0.wrec
    nc = bass.Bass("TRN2", target_bir_lowering=False)
    with ExitStack() as es:
        t = declare(nc, es)
        S = Sched(dry=False)
        P = Prog(nc, S, t, wseq)
        P.run()
        assert P.wi == len(wseq)
        sems = {name: es.enter_context(nc.semaphore(name)) for name in S.semcnt}
        block = es.enter_context(nc.Block())
        engmap = {"pe": block.tensor, "act": block.scalar, "dve": block.vector, "pool": block.gpsimd, "sp": block.sync}
        for en in ENGS:
            stream = S.streams[en]

            def body(eng, stream=stream):
                for waits, fn, sem, inc in stream:
                    for s, v in waits:
                        eng.wait_ge(sems[s], v)
                    if fn is not None:
                        fn(eng).then_inc(sems[sem], inc)
            engmap[en](body)
    return nc


class _DryAP:
    def __getitem__(self, k):
        return self

    def __getattr__(self, k):
        return lambda *a, **kw: self


class _DryTens(dict):
    def __missing__(self, k):
        return _DryAP()


def rope_tables():
    rows = TS // 64
    row = np.repeat(np.arange(rows), 64).astype(np.float32)
    col = np.tile(np.arange(64), rows).astype(np.float32)
    half = HD // 2
    inv = (np.float32(10000.0) ** (-np.arange(0, half, 2, dtype=np.float32) / np.float32(half))).astype(np.float32)
    ang = np.concatenate([row[:, None] * inv, col[:, None] * inv], axis=-1).astype(np.float32)
    return np.cos(ang).astype(np.float32), np.sin(ang).astype(np.float32)


def fm(v):
    return np.ascontiguousarray(np.asarray(v, np.float32).reshape(16, 128).T)


def make_in_maps(x_prompt, x_sample, cache_k, cache_v, c, c_ctx, w_mod, b_mod, norm1_g, norm2_g, w_in, q_gain, k_gain,
                 conv_w, w_attn_out, w_conv_out, w_o, w_ff1, w_ff2, final_g):
    f = lambda a: np.ascontiguousarray(np.asarray(a, np.float32))
    cos, sin = rope_tables()
    shared = {
        "w_mod": f(w_mod[0]), "bmodT": np.ascontiguousarray(f(b_mod[0]).reshape(96, 128).T),
        "n1g": fm(norm1_g[0]), "n2g": fm(norm2_g[0]), "fg": fm(final_g),
        "w_in": f(w_in[0]),
        "gq_bc": np.ascontiguousarray(np.broadcast_to(f(q_gain[0])[None, :], (128, 128))),
        "gk_bc": np.ascontiguousarray(np.broadcast_to(f(k_gain[0])[None, :], (128, 128))),
        "convwT": np.ascontiguousarray(f(conv_w[0]).reshape(3, 8, 128).transpose(2, 1, 0)),
        "w_ao": f(w_attn_out[0]), "w_co": f(w_conv_out[0]), "w_o": f(w_o[0]), "w_ff1": f(w_ff1[0]), "w_ff2": f(w_ff2[0]),
        "ident": np.eye(128, dtype=np.float32), "rope_cos": cos, "rope_sin": sin,
    }
    xs = f(x_sample); xp = f(x_prompt); ck = f(cache_k); cv = f(cache_v); cc = f(c); cx = f(c_ctx)
    maps = []
    for b in range(8):
        cond = np.stack([cc[b], cx], axis=0)
        condT = np.ascontiguousarray(cond.reshape(2, 16, 128).transpose(2, 1, 0))
        m = dict(shared)
        m.update({
            "xs": xs[b], "xp": np.ascontiguousarray(xp[4 * b:4 * b + 4].reshape(TP, D)),
            "ck": np.ascontiguousarray(ck[b, 0].reshape(256, 256)), "cv": np.ascontiguousarray(cv[b, 0].reshape(256, 256)),
            "condT": condT,
        })
        maps.append(m)
    return maps


_NC = None


def kernel(**inputs):
    global _NC
    maps = make_in_maps(**inputs)
    if _NC is None:
        _NC = build()
    res = run_bass_kernel_spmd(_NC, maps, core_ids=list(range(8)))
    r = res.results
    y_sample = np.stack([r[b]["ys"] for b in range(8)], axis=0).astype(np.float32)
    y_prompt = np.concatenate([r[b]["yp"].reshape(4, 256, D) for b in range(8)], axis=0).astype(np.float32)
    new_k = np.concatenate([r[b]["nk"].reshape(4, 1, 256, NKV, HD) for b in range(8)], axis=0).astype(np.float32)
    new_v = np.concatenate([r[b]["nv"].reshape(4, 1, 256, NKV, HD) for b in range(8)], axis=0).astype(np.float32)
    return (y_prompt, y_sample, new_k, new_v)
```

```python
import numpy as np
from contextlib import ExitStack
import concourse.bass as bass
import concourse.mybir as mybir
from concourse.bass_utils import run_bass_kernel_spmd

F32 = mybir.dt.float32
BF16 = mybir.dt.bfloat16
AF = mybir.ActivationFunctionType
ALU = mybir.AluOpType
AX = mybir.AxisListType

D = 2048
NCH = 16
TS = 4096
TP = 1024
T = 512
HD = 128
NH = 8
NKV = 2
INW = 8704
DFF = 8192
EPS = 1e-6
SCALE = HD ** -0.5
NS = 5
ENGS = ("pe", "act", "dve", "pool", "sp")
MRES = ["mods", "gm1", "gm2"] + [("mod", j) for j in range(96)]

WT = []
for i in range(34):
    WT.append(("w_in", 0, 16, i * 256, 256))
for i in range(8):
    WT.append(("w_ao", 0, 8, i * 256, 256))
for i in range(8):
    WT.append(("w_co", 0, 8, i * 256, 256))
for i in range(8):
    WT.append(("w_o", 0, 16, i * 256, 256))
for i in range(32):
    WT.append(("w_ff1", 0, 16, i * 256, 256))
for j in range(16):
    for hf in range(2):
        WT.append(("w_ff2", hf * 32, 32, j * 128, 128))
I_IN, I_AO, I_CO, I_WO, I_F1, I_F2 = 0, 34, 42, 50, 58, 90
NWT = len(WT)
GROUPS = [[4, 5], list(range(0, 4)) + list(range(6, 18)), list(range(34, 50)) + list(range(18, 34)),
          list(range(50, 90)), list(range(90, 122))]
GRP_OF = {sx: g for g, lst in enumerate(GROUPS) for sx in lst}


class Sched:
    def __init__(self, dry):
        self.dry = dry
        self.streams = {e: [] for e in ENGS}
        self.semcnt = {}
        self.seen = {e: {} for e in ENGS}
        self.lastw = {}
        self.readers = {}

    def op(self, eng, fn, reads=(), writes=(), dma=None):
        if self.dry:
            return
        own = "c_" + eng
        raw = {}
        oth = {}
        for r in reads:
            t = self.lastw.get(r)
            if t is not None and raw.get(t[0], 0) < t[1]:
                raw[t[0]] = t[1]
        for w in writes:
            t = self.lastw.get(w)
            if t is not None and oth.get(t[0], 0) < t[1]:
                oth[t[0]] = t[1]
            for s, v in self.readers.get(w, {}).items():
                if oth.get(s, 0) < v:
                    oth[s] = v
        deps = dict(raw)
        for s, v in oth.items():
            if deps.get(s, 0) < v:
                deps[s] = v
        if eng == "pe":
            deps.pop(own, None)
        waits = []
        seen = self.seen[eng]
        for s, v in deps.items():
            if seen.get(s, 0) < v:
                seen[s] = v
                waits.append((s, v))
        if dma is None:
            sem, inc = own, 1
        else:
            sem, inc = "d_" + dma, 16
        val = self.semcnt.get(sem, 0) + inc
        self.semcnt[sem] = val
        self.streams[eng].append((waits, fn, sem, inc))
        for r in reads:
            d = self.readers.setdefault(r, {})
            if d.get(sem, 0) < val:
                d[sem] = val
        for w in writes:
            self.lastw[w] = (sem, val)
            self.readers[w] = {}

    def finish(self):
        if self.dry:
            return
        waits = [(s, v) for s, v in self.semcnt.items()]
        self.streams["sp"].append((waits, None, None, 0))


def rrows(off, nbytes):
    return [("R", k) for k in range(off // 1024, (off + nbytes - 1) // 1024 + 1)]


class Prog:
    def __init__(self, nc, S, tens, wseq):
        self.nc = nc
        self.S = S
        self.t = tens
        self.wseq = wseq
        self.wrec = []
        self.wi = 0
        self.nb = 0
        self.pool = list(range(8))
        self.ptn = 0
        self.alt = 0
        self.peng = "pool"
        self.npb = 0

    def bank(self):
        b = self.pool[self.nb % len(self.pool)]
        self.nb += 1
        return b

    def R(self, off, nbytes, dt, pattern=None, **kw):
        Rt = self.t["R"]
        v = Rt[:, off // 2:(off + nbytes) // 2]
        if dt == F32:
            v = v.bitcast(F32)
        if pattern is not None:
            v = v.rearrange(pattern, **kw)
        return v, rrows(off, nbytes)

    def ring(self, slot):
        return self.t["ring"][:, slot, :]

    def wload(self, n):
        if n >= len(self.wseq):
            return
        sidx = self.wseq[n]
        slot = n % NS
        _, _, nkc, _, ncols = WT[sidx]
        nv = nkc * ncols
        dst = self.ring(slot)[:, 0:nv]
        src = self.t["wscr"][sidx][:, 0:nv]
        self.S.op("sp", lambda e: e.dma_start(out=dst, in_=src),
                  reads=[("wscr", sx) for sx in GROUPS[GRP_OF[sidx]]], writes=[("ring", slot)], dma="ring%d" % slot)

    def wreq(self, sidx):
        n = self.wi
        self.wi += 1
        if self.wseq is None:
            self.wrec.append(sidx)
        else:
            assert self.wseq[n] == sidx, (n, self.wseq[n], sidx)
        slot = n % NS
        _, kc0, nkc, col0, ncols = WT[sidx]
        v = self.ring(slot)[:, 0:nkc * ncols].rearrange("p (k n) -> p k n", k=nkc)
        return v, ("ring", slot), n

    def wrel(self, n):
        if self.wseq is not None:
            self.wload(n + NS)

    def cp(self, eng, out, in_):
        if eng == "act":
            return lambda e: e.activation(out=out, in_=in_, func=AF.Copy)
        return lambda e: e.tensor_copy(out, in_)

    def alt2(self, a="act", b="dve"):
        self.alt += 1
        return a if self.alt % 2 else b

    def mmgroup(self, out, pairs):
        n = len(pairs)

        def f(e):
            ins = None
            for i, (l, r) in enumerate(pairs):
                ins = e.matmul(out, l, r, start=(i == 0), stop=(i == n - 1))
            return ins
        return f

    def mods_chunks(self, js, stg_offs):
        S, t = self.S, self.t
        wm = t["d_w_mod"].rearrange("(k p) n -> p k n", p=128)
        mod, sc = t["mod"], t["silu"]
        for i, j in enumerate(js):
            k = i % len(stg_offs)
            stg, sres = self.R(stg_offs[k], 8192, F32, "p (k n) -> p k n", k=16)
            S.op("sp", lambda e, stg=stg, j=j: e.dma_start(out=stg, in_=wm[:, :, j * 128:(j + 1) * 128]),
                 writes=sres, dma="mstg%d" % (stg_offs[k] // 8192))
            b = self.bank()
            pairs = [(stg[:, kc, :], sc[:, kc, :]) for kc in range(16)]
            S.op("pe", self.mmgroup(t["ps"][:, b, 0:2], pairs), reads=sres + ["silu"], writes=[("ps", b)])
            S.op("dve", lambda e, b=b, j=j: e.tensor_scalar(out=mod[:, j, :], in0=t["ps"][:, b, 0:2],
                                                             scalar1=t["c_bmodT"][:, j:j + 1], scalar2=None,
                                                             op0=ALU.add),
                 reads=[("ps", b), "bmodT"], writes=[("mod", j)])

    def mods_all(self):
        S, t = self.S, self.t
        wm = t["d_w_mod"].rearrange("(k p) n -> p k n", p=128)
        mod, sc, ps = t["mod"], t["silu"], t["ps"]
        tm = t["rl"][0:2, 0, :]
        last = []
        for cg in range(24):
            k = cg % 2
            stg, sres = self.R(k * 32768, 32768, F32, "p (k n) -> p k n", k=16)
            S.op("sp", lambda e, stg=stg, cg=cg: e.dma_start(out=stg, in_=wm[:, :, cg * 512:(cg + 1) * 512]),
                 writes=sres, dma="mstg%d" % k)
            if cg >= 22:
                last += sres
            b = self.bank()
            S.op("pe", self.mmgroup(ps[0:2, b, :], [(sc[:, kc, :], stg[:, kc, :]) for kc in range(16)]),
                 reads=sres + ["silu"], writes=[("ps", b)])
            S.op("dve", lambda e, b=b: e.tensor_copy(tm, ps[0:2, b, :]), reads=[("ps", b)], writes=[("rl", 0)])
            b2 = self.bank()

            def f(e, b2=b2):
                ins = None
                for k4 in range(4):
                    ins = e.transpose(ps[:, b2, k4 * 2:(k4 + 1) * 2], tm[:, k4 * 128:(k4 + 1) * 128], t["c_ident"][0:2, 0:2])
                return ins
            S.op("pe", f, reads=[("rl", 0), "ident"], writes=[("ps", b2)])
            S.op("dve", lambda e, b2=b2, cg=cg: e.tensor_tensor(
                out=mod[:, cg * 4:(cg + 1) * 4, :], in0=ps[:, b2, 0:8].rearrange("p (j r) -> p j r", j=4),
                in1=t["c_bmodT"][:, cg * 4:(cg + 1) * 4].unsqueeze(2).broadcast_to([128, 4, 2]), op=ALU.add),
                reads=[("ps", b2), "bmodT"], writes=[("mod", j) for j in range(cg * 4, cg * 4 + 4)])
        return last

    def mods_gm(self, gm, sl, ng, grp):
        S, t = self.S, self.t
        g, mod = t[gm], t["mod"]
        S.op("dve", lambda e: e.tensor_scalar(out=g[:], in0=mod[:, sl:sl + 16, :], scalar1=1.0, scalar2=None, op0=ALU.add),
             reads=[("mod", j) for j in range(sl, sl + 16)], writes=[gm])
        S.op("dve", lambda e: e.tensor_tensor(out=g[:], in0=g[:], in1=t["c_" + ng][:].unsqueeze(2).broadcast_to([128, 16, 2]),
                                              op=ALU.mult), reads=[gm, ng], writes=[gm])

    def prologue(self):
        S, t, nc = self.S, self.t, self.nc
        for name in ("ident", "condT", "bmodT", "n1g", "n2g", "fg", "gq_bc", "gk_bc", "convwT"):
            dst = t["c_" + name]
            src = t["d_" + name]
            S.op("sp", lambda e, dst=dst, src=src: e.dma_start(out=dst[:], in_=src),
                 writes=[name], dma=name)
        S.op("dve", lambda e: e.memset(t["ones_b"][:], 1.0), writes=["ones_b"])
        S.op("dve", lambda e: e.memset(t["ones_mean"][:], 1.0 / D), writes=["ones_mean"])
        S.op("dve", lambda e: e.memset(t["ones_f"][:], 1.0), writes=["ones_f"])
        S.op("dve", lambda e: e.memset(t["eps"][:], EPS), writes=["eps"])
        S.op("dve", lambda e: e.tensor_copy(t["ident_b"][:], t["c_ident"][:]), reads=["ident"], writes=["ident_b"])

        def convert(g, extra_reads):
            for sidx in GROUPS[g]:
                name, kc0, nkc, col0, ncols = WT[sidx]
                src = t["d_" + name].rearrange("(k p) n -> p k n", p=128)[:, kc0:kc0 + nkc, col0:col0 + ncols]
                dst = t["wscr"][sidx][:, 0:nkc * ncols].rearrange("p (k n) -> p k n", k=nkc)
                S.op("pool", lambda e, dst=dst, src=src: e.dma_start(out=dst, in_=src),
                     reads=extra_reads, writes=[("wscr", sidx)], dma="cvg%d" % g)
        for g in range(len(GROUPS)):
            convert(g, [])
        sc = t["silu"]
        S.op("act", lambda e: e.activation(out=sc[:], in_=t["c_condT"][:], func=AF.Silu),
             reads=["condT"], writes=["silu"])
        last = self.mods_all()
        self.mods_gm("gm1", 16, "n1g", 1)
        self.mods_gm("gm2", 64, "n2g", 4)

        ckf, ckr = self.R(0, 2048, F32, "p (b n) -> p b n", b=2)
        cvf, cvr = self.R(2048, 2048, F32, "p (b n) -> p b n", b=2)
        ckb, ckbr = self.R(4096, 1024, BF16, "p (b n) -> p b n", b=2)
        S.op("sp", lambda e: e.dma_start(out=ckf, in_=t["d_ck"].rearrange("(b p) n -> p b n", p=128)),
             writes=ckr, dma="ckf")
        S.op("sp", lambda e: e.dma_start(out=cvf, in_=t["d_cv"].rearrange("(b p) n -> p b n", p=128)),
             writes=cvr, dma="cvf")
        S.op("dve", lambda e: e.tensor_copy(ckb, ckf), reads=ckr, writes=ckbr)
        S.op("act", self.cp("act", t["Vc"][:], cvf), reads=cvr, writes=["Vc"])
        b = self.bank()
        psb = t["ps"][:, b, :].bitcast(BF16)

        def ftr(e):
            ins = None
            for blk in range(2):
                for g in range(2):
                    ins = e.transpose(psb[:, (g * 2 + blk) * 128:(g * 2 + blk + 1) * 128],
                                      ckb[:, blk, g * 128:(g + 1) * 128], t["ident_b"][:])
            return ins
        S.op("pe", ftr, reads=ckbr + ["ident_b"], writes=[("ps", b)])
        S.op("dve", lambda e: e.tensor_copy(t["KTc"][:].rearrange("p g n -> p (g n)"), psb[:, 0:512]),
             reads=[("ps", b)], writes=["KTc"])
        sqk, sqr = self.R(8192, 2048, F32)
        S.op("dve", lambda e: e.tensor_tensor(out=sqk, in0=t["KTc"][:].rearrange("p g n -> p (g n)"),
                                              in1=t["KTc"][:].rearrange("p g n -> p (g n)"), op=ALU.mult),
             reads=["KTc"], writes=sqr)
        b2 = self.bank()
        S.op("pe", lambda e: e.matmul(t["ps"][:, b2, :], t["ones_f"][:], sqk, start=True, stop=True),
             reads=sqr + ["ones_f"], writes=[("ps", b2)])
        sm = t["small"]
        S.op("dve", lambda e: e.tensor_reduce(out=sm[:, 0:1], in_=t["ps"][:, b2, :], axis=AX.X, op=ALU.max),
             reads=[("ps", b2)], writes=["small"])
        S.op("dve", lambda e: e.tensor_reduce(out=sm[:, 1:2], in_=t["c_gq_bc"][:], axis=AX.X, op=ALU.max,
                                              apply_absolute_value=True), reads=["gq_bc"], writes=["small"])
        S.op("dve", lambda e: e.tensor_reduce(out=sm[:, 2:3], in_=t["c_gk_bc"][:], axis=AX.X, op=ALU.max,
                                              apply_absolute_value=True), reads=["gk_bc"], writes=["small"])
        S.op("dve", lambda e: e.scalar_tensor_tensor(out=sm[:, 3:4], in0=sm[:, 2:3], scalar=float(HD),
                                                     in1=sm[:, 2:3], op0=ALU.mult, op1=ALU.mult),
             reads=["small"], writes=["small"])
        S.op("dve", lambda e: e.tensor_tensor(out=sm[:, 4:5], in0=sm[:, 3:4], in1=sm[:, 0:1], op=ALU.max),
             reads=["small"], writes=["small"])
        S.op("dve", lambda e: e.scalar_tensor_tensor(out=sm[:, 5:6], in0=sm[:, 1:2], scalar=sm[:, 1:2],
                                                     in1=sm[:, 4:5], op0=ALU.mult, op1=ALU.mult),
             reads=["small"], writes=["small"])
        S.op("act", lambda e: e.activation(out=sm[:, 6:7], in_=sm[:, 5:6], func=AF.Sqrt),
             reads=["small"], writes=["small"])
        S.op("dve", lambda e: e.tensor_scalar(out=t["negM"][:], in0=sm[:, 6:7], scalar1=-1.02, scalar2=None,
                                              op0=ALU.mult), reads=["small"], writes=["negM"])
        if self.wseq is not None:
            for n in range(NS):
                self.wload(n)

    def load_x(self, src, t0):
        for blk in range(4):
            self.load_x_blk(src, t0, blk)

    def norm(self, gm, sh, cond, mode, have=None):
        S, t = self.S, self.t
        xT, ps, h = t["xT"], t["ps"], t["h"]
        xres = [("xT", c) for c in range(16)]
        sq, sqres = self.R(0, 32768, F32, "p (c n) -> p c n", c=16)
        ssum, ssres = self.R(32768 + 4096, 2048, F32)
        rstd, rsres = self.R(32768 + 6144, 2048, F32)
        if have != "rstd":
            if have == "acc":
                src_sum, src_res = t["rl"][:, 1, :], [("rl", 1)]
            else:
                S.op("act", lambda e: e.activation(out=sq, in_=xT[:], func=AF.Square), reads=xres, writes=sqres)
                S.op("dve", lambda e: e.tensor_reduce(out=ssum, in_=sq.rearrange("p c n -> p n c"), axis=AX.X, op=ALU.add),
                     reads=sqres, writes=ssres)
                src_sum, src_res = ssum, ssres
            b = self.bank()
            S.op("pe", lambda e: e.matmul(ps[:, b, :], t["ones_mean"][:], src_sum, start=True, stop=True),
                 reads=src_res + ["ones_mean"], writes=[("ps", b)])
            S.op("act", lambda e: e.activation(out=ssum, in_=ps[:, b, :], func=AF.Sqrt, bias=t["eps"][:, 0:1], scale=1.0),
                 reads=[("ps", b), "eps"], writes=ssres)
            S.op("dve", lambda e: e.reciprocal(rstd, ssum), reads=ssres, writes=rsres)
        for c in range(16):
            gsc = gm[:, c, cond:cond + 1]
            if mode == "h":
                tmp, tres = self.R(32768 + (c % 2) * 2048, 2048, F32)
                S.op("dve", lambda e, c=c, tmp=tmp, gsc=gsc: e.scalar_tensor_tensor(
                    out=tmp, in0=xT[:, c, :], scalar=gsc, in1=rstd, op0=ALU.mult, op1=ALU.mult),
                    reads=[("xT", c)] + rsres + MRES, writes=tres)
                S.op("act", lambda e, c=c, tmp=tmp: e.activation(
                    out=h[:, c, 0:512], in_=tmp, func=AF.Identity, bias=sh[:, c, cond:cond + 1], scale=1.0),
                    reads=tres + MRES, writes=[("h", c)])
            else:
                S.op("dve", lambda e, c=c, gsc=gsc: e.scalar_tensor_tensor(
                    out=sq[:, c, :], in0=xT[:, c, :], scalar=gsc, in1=rstd, op0=ALU.mult, op1=ALU.mult),
                    reads=[("xT", c)] + rsres + MRES, writes=rrows(c * 2048, 2048))

    def halo(self, src, t0, ntok, gm, sh, cond):
        S, t = self.S, self.t
        ps, h, xh = t["ps"], t["h"], t["xh"]
        tl = max(t0 - 1, 0)
        tr = min(t0 + T, ntok - 1)
        S.op("sp", lambda e: e.dma_start(out=xh[:, 0:128], in_=src[tl, :].rearrange("(c p) -> c p", p=128)),
             writes=["xh"], dma="xh")
        S.op("sp", lambda e: e.dma_start(out=xh[:, 128:256], in_=src[tr, :].rearrange("(c p) -> c p", p=128)),
             writes=["xh"], dma="xh")
        b = self.bank()

        def f(e):
            e.transpose(ps[:, b, 0:16], xh[:, 0:128], t["c_ident"][0:16, 0:16])
            return e.transpose(ps[:, b, 16:32], xh[:, 128:256], t["c_ident"][0:16, 0:16])
        S.op("pe", f, reads=["xh", "ident"], writes=[("ps", b)])
        hs = t["hsm"]
        xhT = hs[:, 0:32].rearrange("p (t c) -> p t c", t=2)
        sqh = hs[:, 32:64].rearrange("p (t c) -> p t c", t=2)
        ssh = hs[:, 64:66]
        rsh = hs[:, 66:68]
        th = hs[:, 96:128].rearrange("p (t c) -> p t c", t=2)
        S.op("dve", lambda e: e.tensor_copy(hs[:, 0:32], ps[:, b, 0:32]), reads=[("ps", b)], writes=["hsm"])
        S.op("dve", lambda e: e.tensor_tensor(out=sqh, in0=xhT, in1=xhT, op=ALU.mult), reads=["hsm"], writes=["hsm"])
        S.op("dve", lambda e: e.tensor_reduce(out=ssh, in_=sqh, axis=AX.X, op=ALU.add), reads=["hsm"], writes=["hsm"])
        b2 = self.bank()
        S.op("pe", lambda e: e.matmul(ps[:, b2, 0:2], t["ones_mean"][:], ssh, start=True, stop=True),
             reads=["hsm", "ones_mean"], writes=[("ps", b2)])
        S.op("act", lambda e: e.activation(out=ssh, in_=ps[:, b2, 0:2], func=AF.Sqrt, bias=t["eps"][:, 0:1], scale=1.0),
             reads=[("ps", b2), "eps"], writes=["hsm"])
        S.op("dve", lambda e: e.reciprocal(rsh, ssh), reads=["hsm"], writes=["hsm"])
        S.op("dve", lambda e: e.tensor_tensor(out=th, in0=xhT, in1=gm[:, :, cond].unsqueeze(1).broadcast_to([128, 2, 16]),
                                              op=ALU.mult), reads=["hsm"] + MRES, writes=["hsm"])
        S.op("dve", lambda e: e.tensor_tensor(out=th, in0=th, in1=rsh.unsqueeze(2).broadcast_to([128, 2, 16]),
                                              op=ALU.mult), reads=["hsm"], writes=["hsm"])
        S.op("dve", lambda e: e.tensor_tensor(out=t["hh"][:], in0=th.rearrange("p t c -> p c t"),
                                              in1=sh[:, :, cond:cond + 1].broadcast_to([128, 16, 2]), op=ALU.add),
             reads=["hsm"] + MRES, writes=["hhalo"])

    def load_rope(self, t0):
        S, t = self.S, self.t
        for nm in ("cos", "sin"):
            dst = t["rope_" + nm]
            src = t["d_rope_" + nm][t0:t0 + T, :].rearrange("(b p) n -> p b n", p=128)
            S.op("sp", lambda e, dst=dst, src=src: e.dma_start(out=dst[:], in_=src), writes=["rope_" + nm],
                 dma="rope_" + nm)

    def tokproj(self, sidx, dst_fn, dres_fn):
        S, t = self.S, self.t
        ps, h = t["ps"], t["h"]
        w, wres, wn = self.wreq(sidx)
        hres = [("h", c) for c in range(16)]
        for blk in range(4):
            b = self.bank()
            pairs = [(h[:, kc, blk * 128:(blk + 1) * 128], w[:, kc, :]) for kc in range(16)]
            S.op("pe", self.mmgroup(ps[:, b, 0:256], pairs), reads=[wres] + hres, writes=[("ps", b)])
            eng = self.alt2()
            S.op(eng, self.cp(eng, dst_fn(blk), ps[:, b, 0:256]), reads=[("ps", b)], writes=dres_fn(blk))
        self.wrel(wn)

    def headnorm_g(self, src4, sres, nh, gain_bc, gname, rope, dst_bf, dres, dst_f32=None, d32res=None):
        S, t = self.S, self.t
        G = 4 * nh
        n = G * 128
        S1, S1r = self.R(32768, n * 4, F32, "p (g d) -> p g d", g=G)
        sm = t["small2"]
        ssq = sm[:, 0:G]
        rs = sm[:, 32:32 + G]
        S.op("dve", lambda e: e.tensor_tensor(out=S1.rearrange("p (b h) d -> p b h d", b=4), in0=src4, in1=src4, op=ALU.mult),
             reads=sres, writes=S1r)
        yield
        S.op("dve", lambda e: e.tensor_reduce(out=ssq, in_=S1, axis=AX.X, op=ALU.add), reads=S1r, writes=["small2"])
        yield
        S.op("act", lambda e: e.activation(out=ssq, in_=ssq, func=AF.Sqrt, bias=t["eps"][:, 0:1], scale=1.0 / HD),
             reads=["small2", "eps"], writes=["small2"])
        yield
        S.op("dve", lambda e: e.reciprocal(rs, ssq), reads=["small2"], writes=["small2"])
        yield
        rsb = rs.rearrange("p (b h) -> p b h", b=4).unsqueeze(3).broadcast_to([128, 4, nh, 128])
        S.op("dve", lambda e: e.tensor_tensor(out=src4, in0=src4, in1=rsb, op=ALU.mult), reads=sres + ["small2"], writes=sres)
        yield
        gb = gain_bc[:].unsqueeze(1).unsqueeze(1).broadcast_to([128, 4, nh, 128])
        if not rope:
            if dst_f32 is not None:
                S.op(self.peng, lambda e: e.tensor_tensor(out=dst_f32, in0=src4, in1=gb, op=ALU.mult),
                     reads=sres + [gname], writes=d32res)
                yield
                S.op("act", self.cp("act", dst_bf, dst_f32), reads=d32res, writes=dres)
                yield
            else:
                S.op(self.peng, lambda e: e.tensor_tensor(out=dst_bf, in0=src4, in1=gb, op=ALU.mult),
                     reads=sres + [gname], writes=dres)
                yield
            return
        S.op(self.peng, lambda e: e.tensor_tensor(out=src4, in0=src4, in1=gb, op=ALU.mult), reads=sres + [gname], writes=sres)
        yield
        x0 = src4[:, :, :, 0::2]
        x1 = src4[:, :, :, 1::2]
        cs = t["rope_cos"][:].unsqueeze(2).broadcast_to([128, 4, nh, 64])
        sn = t["rope_sin"][:].unsqueeze(2).broadcast_to([128, 4, nh, 64])
        hb = G * 64 * 4
        T1, T1r = self.R(32768, hb, F32, "p (b h d) -> p b h d", b=4, h=nh)
        T2, T2r = self.R(32768 + hb, hb, F32, "p (b h d) -> p b h d", b=4, h=nh)
        pe_ = self.peng
        S.op("dve", lambda e: e.tensor_tensor(out=T1, in0=x0, in1=cs, op=ALU.mult), reads=sres + ["rope_cos"], writes=T1r)
        yield
        S.op(pe_, lambda e: e.tensor_tensor(out=T2, in0=x1, in1=sn, op=ALU.mult), reads=sres + ["rope_sin"], writes=T2r)
        yield
        S.op("dve", lambda e: e.tensor_tensor(out=dst_bf[:, :, :, 0::2], in0=T1, in1=T2, op=ALU.subtract),
             reads=T1r + T2r, writes=dres)
        yield
        S.op(pe_, lambda e: e.tensor_tensor(out=T1, in0=x0, in1=sn, op=ALU.mult), reads=sres + ["rope_sin"], writes=T1r)
        yield
        S.op("dve", lambda e: e.tensor_tensor(out=T2, in0=x1, in1=cs, op=ALU.mult), reads=sres + ["rope_cos"], writes=T2r)
        yield
        S.op(pe_, lambda e: e.tensor_tensor(out=dst_bf[:, :, :, 1::2], in0=T1, in1=T2, op=ALU.add),
             reads=T1r + T2r, writes=dres)
        yield

    def tposeT_g(self, src_bf, sres, nh, dst_fn, dres):
        S, t = self.S, self.t
        for blk in range(4):
            b = self.bank()
            psb = t["ps"][:, b, :].bitcast(BF16)

            def f(e, blk=blk, psb=psb):
                ins = None
                for hd in range(nh):
                    ins = e.transpose(psb[:, hd * 128:(hd + 1) * 128], src_bf[:, blk, hd, :], t["ident_b"][:])
                return ins
            S.op("pe", f, reads=sres + ["ident_b"], writes=[("ps", b)])
            yield
            eng = self.alt2()
            S.op(eng, self.cp(eng, dst_fn(blk), psb[:, 0:nh * 128].rearrange("p (h n) -> p h n", h=nh)),
                 reads=[("ps", b)], writes=dres)
            yield

    def headnorm(self, *a, **kw):
        for _ in self.headnorm_g(*a, **kw):
            pass

    def tposeT(self, *a, **kw):
        for _ in self.tposeT_g(*a, **kw):
            pass

    def kv_phase1(self, it, between=None):
        S, t = self.S, self.t
        t0 = it * T
        kvf, kvr = self.R(16384, 8192, F32, "p (b n) -> p b n", b=4)
        self.tokproj(I_IN + 4, lambda blk: kvf[:, blk, 0:256], lambda blk: kvr)
        self.tokproj(I_IN + 5, lambda blk: t["Vs"][:, it * 4 + blk, :], lambda blk: [("Vs", it * 4 + blk)])
        kr, krr = self.R(0, 2048, BF16, "p (b h d) -> p b h d", b=4, h=2)

        def chain():
            yield from self.headnorm_g(kvf[:, :, 0:256].rearrange("p b (h d) -> p b h d", h=2), kvr, 2, t["c_gk_bc"],
                                       "gk_bc", True, kr, krr)
            yield from self.tposeT_g(kr, krr, 2, lambda blk: t["KTs"][:, :, t0 + blk * 128:t0 + (blk + 1) * 128],
                                     [("KTs", it)])
        gen = chain()
        if between is not None:
            src, nt0 = between
            for blk in range(3):
                self.x_dma(src, nt0, blk, blk)
            for blk in range(4):
                self.load_x_blk(src, nt0, blk, k=(0, 1, 2, 0)[blk], do_dma=False)
                if blk == 0:
                    self.x_dma(src, nt0, 3, 0)
                for _ in range(4):
                    next(gen, None)
        for _ in gen:
            pass

    def q_proj(self, sample):
        qf, qfr = self.R(0, 16384, F32, "p (b n) -> p b n", b=4)
        for i in range(4):
            self.tokproj(I_IN + i, lambda blk, i=i: qf[:, blk, i * 256:(i + 1) * 256], lambda blk: qfr)
        if not sample:
            kvf, kvr = self.R(16384, 8192, F32, "p (b n) -> p b n", b=4)
            self.tokproj(I_IN + 4, lambda blk: kvf[:, blk, 0:256], lambda blk: kvr)
            self.tokproj(I_IN + 5, lambda blk: kvf[:, blk, 256:512], lambda blk: kvr)

    def q_stage_g(self, sample, t0, it):
        S, t = self.S, self.t
        qf, qfr = self.R(0, 16384, F32, "p (b h d) -> p b h d", b=4, h=8)
        qT, qTr = self.R(24576, 8192, BF16, "p (h n) -> p h n", h=8)
        qr, qrr = self.R(49152, 8192, BF16, "p (b h d) -> p b h d", b=4, h=8)
        yield from self.headnorm_g(qf, qfr, 8, t["c_gq_bc"], "gq_bc", sample, qr, qrr)
        yield from self.tposeT_g(qr, qrr, 8, lambda blk: qT[:, :, blk * 128:(blk + 1) * 128], qTr)
        if not sample:
            kvf, kvr = self.R(16384, 8192, F32, "p (b n) -> p b n", b=4)
            kout, koutr = self.R(57344, 4096, F32, "p (b h d) -> p b h d", b=4, h=2)
            KTp, KTpr = self.R(61440, 2048, BF16, "p (g n) -> p g n", g=2)
            Vp, Vpr = self.R(63488, 2048, BF16, "p (b n) -> p b n", b=4)
            kr, krr = self.R(0, 2048, BF16, "p (b h d) -> p b h d", b=4, h=2)
            yield from self.headnorm_g(kvf[:, :, 0:256].rearrange("p b (h d) -> p b h d", h=2), kvr, 2, t["c_gk_bc"], "gk_bc", False,
                          kr, krr, dst_f32=kout, d32res=koutr)
            yield from self.tposeT_g(kr, krr, 2, lambda blk: KTp[:, :, blk * 128:(blk + 1) * 128], KTpr)
            S.op("dve", lambda e: e.tensor_copy(Vp, kvf[:, :, 256:512]), reads=kvr, writes=Vpr)
            S.op("sp", lambda e: e.dma_start(out=t["o_nk"][t0:t0 + T, :].rearrange("(b p) n -> p b n", p=128),
                                             in_=kout.rearrange("p b h d -> p b (h d)")),
                 reads=koutr, writes=[("nk", t0)], dma="kout")
            S.op("sp", lambda e: e.dma_start(out=t["o_nv"][t0:t0 + T, :].rearrange("(b p) n -> p b n", p=128),
                                             in_=kvf[:, :, 256:512]),
                 reads=kvr, writes=[("nv", t0)], dma="vout")

    def pairbank(self):
        p = (4, 6)[self.npb % 2]
        self.npb += 1
        return p

    def attn_head(self, sample, hd, c0, n, nkc, bo, qT, qTr, aT, pk):
        S, t = self.S, self.t
        ps = t["ps"]
        g = hd // 4
        npair = nkc // 2
        bs = 2 + bo
        if pk is not None:
            KTp, KTpr, Vp, Vpr = pk

        def kv(kc):
            if sample:
                if kc < 32:
                    return (t["KTs"][:, g, kc * 128:(kc + 1) * 128], [("KTs", kc // 4)],
                            t["Vs"][:, kc, g * 128:(g + 1) * 128], [("Vs", kc)])
                return (t["KTc"][:, g, (kc - 32) * 128:(kc - 31) * 128], ["KTc"],
                        t["Vc"][:, kc - 32, g * 128:(g + 1) * 128], ["Vc"])
            blk = c0 // 128 + kc
            return (KTp[:, g, blk * 128:(blk + 1) * 128], KTpr, Vp[:, blk, g * 128:(g + 1) * 128], Vpr)
        acc, accr = self.R(40960, 2048, F32)

        def s_op(kp):
            kT0, kr0, _, _ = kv(2 * kp)
            kT1, kr1, _, _ = kv(2 * kp + 1)
            b = self.pairbank()

            def f(e):
                e.matmul(ps[:, b, 0:n], kT0, qT[:, hd, c0:c0 + n], start=True, stop=True)
                return e.matmul(ps[:, b + 1, 0:n], kT1, qT[:, hd, c0:c0 + n], start=True, stop=True)
            S.op("pe", f, reads=list(kr0) + list(kr1) + qTr, writes=[("ps", b), ("ps", b + 1)])
            return b

        def pv_op(kp, b):
            _, _, v0, vr0 = kv(2 * kp)
            _, _, v1, vr1 = kv(2 * kp + 1)
            k = self.ptn % 4
            self.ptn += 1
            pt, ptr = self.R(32768 + k * 2048, 2048, BF16, "p (a n) -> p a n", a=2)
            S.op("act", lambda e: e.activation(out=pt[:, :, 0:n], in_=ps[:, b:b + 2, 0:n], func=AF.Exp,
                                               bias=t["negM"][:, 0:1], scale=SCALE),
                 reads=[("ps", b), ("ps", b + 1), "negM"], writes=ptr)

            def f(e):
                e.matmul(ps[:, bo, 0:n], v0, pt[:, 0, 0:n], start=(kp == 0), stop=False)
                e.matmul(ps[:, bo, 0:n], v1, pt[:, 1, 0:n], start=False, stop=(kp == npair - 1))
                return e.matmul(ps[:, bs, 0:n], t["ones_b"][:], pt[:, 1, 0:n], start=(kp == 0), stop=False)
            S.op("pe", f, reads=ptr + list(vr0) + list(vr1) + ["ones_b"], writes=[("ps", bo), ("ps", bs)])
            if kp == 0:
                S.op("dve", lambda e: e.tensor_copy(acc[:, 0:n], pt[:, 0, 0:n]), reads=ptr, writes=accr)
            else:
                S.op("dve", lambda e: e.tensor_tensor(out=acc[:, 0:n], in0=acc[:, 0:n], in1=pt[:, 0, 0:n], op=ALU.add),
                     reads=ptr + accr, writes=accr)

        LA = 2
        banks = {}
        for kp in range(min(LA, npair)):
            banks[kp] = s_op(kp)
        for kp in range(npair):
            pv_op(kp, banks.pop(kp))
            if kp + LA < npair:
                banks[kp + LA] = s_op(kp + LA)
        S.op("pe", lambda e: e.matmul(ps[:, bs, 0:n], t["ones_f"][:], acc[:, 0:n], start=False, stop=True),
             reads=accr + ["ones_f"], writes=[("ps", bs)])
        rs, rsr = self.R(49152, 2048, F32)
        S.op("dve", lambda e: e.reciprocal(rs[:, 0:n], ps[:, bs, 0:n]), reads=[("ps", bs)], writes=rsr)
        S.op("dve", lambda e: e.tensor_tensor(out=aT[:, hd, c0:c0 + n], in0=ps[:, bo, 0:n], in1=rs[:, 0:n], op=ALU.mult),
             reads=[("ps", bo)] + rsr, writes=rrows(16384 + hd * 1024, 1024))

    def attention(self, sample):
        S, t = self.S, self.t
        qT, qTr = self.R(24576, 8192, BF16, "p (h n) -> p h n", h=8)
        aT, aTr = self.R(16384, 8192, BF16, "p (h n) -> p h n", h=8)
        if sample:
            segs = [(0, T, 34)]
            pk = None
        else:
            segs = [(0, 256, 2), (256, 256, 2)]
            KTp, KTpr = self.R(61440, 2048, BF16, "p (g n) -> p g n", g=2)
            Vp, Vpr = self.R(63488, 2048, BF16, "p (b n) -> p b n", b=4)
            pk = (KTp, KTpr, Vp, Vpr)
        nacc = 0
        for (c0, n, nkc) in segs:
            for hd in range(NH):
                bo = nacc % 2
                nacc += 1
                self.attn_head(sample, hd, c0, n, nkc, bo, qT, qTr, aT, pk)

    def conv(self, sample, first, last, cond, offs=(32768, 36864, 38912, 24576), side=None, nside=2):
        S, t = self.S, self.t
        ps, h = t["ps"], t["h"]
        hres = [("h", c) for c in range(16)]
        o_cu, o_uf, o_tc, o_cT = offs
        cT, cTr = self.R(o_cT, 8192, BF16, "p (j n) -> p j n", j=8)
        cu, cur = self.R(o_cu, 516 * 4, F32)
        uf, ufr = self.R(o_uf, 2048, F32)
        tc_, tcr = self.R(o_tc, 2048, F32)
        uh = t["hsm"][:, 128:130]
        if sample:
            segs = [(0, T, 1)]
        else:
            segs = [(0, 256, 1), (256, 256, 259)]
            S.op("dve", lambda e: e.memset(cu, 0.0), writes=cur)
        cw = t["c_convwT"]
        for jp in range(4):
            wu, wur, nu = self.wreq(I_IN + 6 + jp)
            wc, wcr, ncc = self.wreq(I_IN + 10 + jp)
            wb, wbr, nbb = self.wreq(I_IN + 14 + jp)
            for jj in range(2):
                j = jp * 2 + jj
                sl = slice(jj * 128, (jj + 1) * 128)
                bu = self.bank()
                S.op("pe", self.mmgroup(ps[:, bu, :], [(wu[:, kc, sl], h[:, kc, 0:512]) for kc in range(16)]),
                     reads=[wur] + hres, writes=[("ps", bu)])
                S.op("act", self.cp("act", uf, ps[:, bu, :]), reads=[("ps", bu)], writes=ufr)
                if sample:
                    bh = self.bank()

                    def fh(e, bh=bh, sl=sl, wu=wu, wc=wc):
                        for kc in range(16):
                            e.matmul(ps[:, bh, 0:2], wu[:, kc, sl], t["hh"][:, kc, :], start=(kc == 0), stop=(kc == 15))
                        ins = None
                        for kc in range(16):
                            ins = e.matmul(ps[:, bh, 2:4], wc[:, kc, sl], t["hh"][:, kc, :], start=(kc == 0), stop=(kc == 15))
                        return ins
                    S.op("pe", fh, reads=[wur, wcr, "hhalo"], writes=[("ps", bh)])
                    S.op("act", self.cp("act", uh, ps[:, bh, 0:2]), reads=[("ps", bh)], writes=["uh"])
                bc = self.bank()
                S.op("pe", self.mmgroup(ps[:, bc, :], [(wc[:, kc, sl], h[:, kc, 0:512]) for kc in range(16)]),
                     reads=[wcr] + hres, writes=[("ps", bc)])
                for (x0, n, c0) in segs:
                    S.op("dve", lambda e, x0=x0, n=n, c0=c0, bc=bc: e.tensor_tensor(
                        out=cu[:, c0:c0 + n], in0=uf[:, x0:x0 + n], in1=ps[:, bc, x0:x0 + n], op=ALU.mult),
                        reads=ufr + [("ps", bc)], writes=cur)
                if sample:
                    S.op("dve", lambda e, bh=bh: e.tensor_tensor(out=cu[:, 0:514:513], in0=uh, in1=ps[:, bh, 2:4], op=ALU.mult),
                         reads=["uh", ("ps", bh)], writes=cur)
                    if first:
                        S.op("dve", lambda e: e.memset(cu[:, 0:1], 0.0), writes=cur)
                    if last:
                        S.op("dve", lambda e: e.memset(cu[:, 513:514], 0.0), writes=cur)
                for (x0, n, c0) in segs:
                    a = c0 - 1
                    S.op("dve", lambda e, x0=x0, n=n, a=a, j=j: e.tensor_scalar(
                        out=tc_[:, x0:x0 + n], in0=cu[:, a:a + n], scalar1=cw[:, j, 0:1], scalar2=None, op0=ALU.mult),
                        reads=cur + ["convwT"], writes=tcr)
                    for k in (1, 2):
                        S.op("dve", lambda e, x0=x0, n=n, a=a, j=j, k=k: e.scalar_tensor_tensor(
                            out=tc_[:, x0:x0 + n], in0=cu[:, a + k:a + k + n], scalar=cw[:, j, k:k + 1],
                            in1=tc_[:, x0:x0 + n], op0=ALU.mult, op1=ALU.add),
                            reads=cur + tcr + ["convwT"], writes=tcr)
                bb = self.bank()
                S.op("pe", self.mmgroup(ps[:, bb, :], [(wb[:, kc, sl], h[:, kc, 0:512]) for kc in range(16)]),
                     reads=[wbr] + hres, writes=[("ps", bb)])
                S.op("dve", lambda e, j=j, bb=bb: e.tensor_tensor(out=cT[:, j, :], in0=tc_, in1=ps[:, bb, :], op=ALU.mult),
                     reads=tcr + [("ps", bb)], writes=rrows(o_cT + j * 1024, 1024))
                if side is not None:
                    for _ in range(nside):
                        next(side, None)
            self.wrel(nu)
            self.wrel(ncc)
            self.wrel(nbb)
        if side is not None:
            for _ in side:
                pass

    def merge(self, o_cT=24576):
        S, t = self.S, self.t
        ps, h = t["ps"], t["h"]
        hres = [("h", c) for c in range(16)]
        aT, aTr = self.R(16384, 8192, BF16, "p (h n) -> p h n", h=8)
        cT, cTr = self.R(o_cT, 8192, BF16, "p (j n) -> p j n", j=8)
        mg, mgr = self.R(0, 16384, BF16, "p (j n) -> p j n", j=16)
        for jp in range(8):
            wa_, war, na = self.wreq(I_AO + jp)
            wc_, wcr, ncx = self.wreq(I_CO + jp)
            wga, wgar, nga = self.wreq(I_IN + 18 + jp)
            wgs, wgsr, ngs = self.wreq(I_IN + 26 + jp)
            for jj in range(2):
                j = jp * 2 + jj
                sl = slice(jj * 128, (jj + 1) * 128)
                bA, bC, ba, bs = self.bank(), self.bank(), self.bank(), self.bank()
                S.op("pe", self.mmgroup(ps[:, bA, :], [(wa_[:, kc, sl], aT[:, kc, :]) for kc in range(8)]),
                     reads=[war] + aTr, writes=[("ps", bA)])
                S.op("pe", self.mmgroup(ps[:, bC, :], [(wc_[:, kc, sl], cT[:, kc, :]) for kc in range(8)]),
                     reads=[wcr] + cTr, writes=[("ps", bC)])
                S.op("pe", self.mmgroup(ps[:, ba, :], [(wga[:, kc, sl], h[:, kc, 0:512]) for kc in range(16)]),
                     reads=[wgar] + hres, writes=[("ps", ba)])
                S.op("pe", self.mmgroup(ps[:, bs, :], [(wgs[:, kc, sl], h[:, kc, 0:512]) for kc in range(16)]),
                     reads=[wgsr] + hres, writes=[("ps", bs)])
                o = 32768 + jj * 8192
                sa, sar = self.R(o, 2048, F32)
                ss, ssr = self.R(o + 2048, 2048, F32)
                t1, t1r = self.R(o + 4096, 2048, F32)
                t2, t2r = self.R(o + 6144, 2048, F32)
                S.op("act", lambda e, sa=sa, ba=ba: e.activation(out=sa, in_=ps[:, ba, :], func=AF.Sigmoid),
                     reads=[("ps", ba)], writes=sar)
                S.op("act", lambda e, ss=ss, bs=bs: e.activation(out=ss, in_=ps[:, bs, :], func=AF.Sigmoid),
                     reads=[("ps", bs)], writes=ssr)
                S.op("dve", lambda e, t1=t1, sa=sa, bA=bA: e.tensor_tensor(out=t1, in0=sa, in1=ps[:, bA, :], op=ALU.mult),
                     reads=sar + [("ps", bA)], writes=t1r)
                S.op("dve", lambda e, t2=t2, ss=ss, bC=bC: e.tensor_tensor(out=t2, in0=ss, in1=ps[:, bC, :], op=ALU.mult),
                     reads=ssr + [("ps", bC)], writes=t2r)
                S.op("pool", lambda e, t1=t1, t2=t2, j=j: e.tensor_tensor(out=mg[:, j, :], in0=t1, in1=t2, op=ALU.add),
                     reads=t1r + t2r, writes=rrows(j * 1024, 1024))
            for n_ in (na, ncx, nga, ngs):
                self.wrel(n_)

    def sq_acc(self, j):
        S, t = self.S, self.t
        rl, xT = t["rl"], t["xT"]
        if j == 0:
            S.op("act", lambda e: e.activation(out=rl[:, 1, :], in_=xT[:, 0, :], func=AF.Square),
                 reads=[("xT", 0)], writes=[("rl", 1)])
            return
        S.op("act", lambda e: e.activation(out=rl[:, 0, :], in_=xT[:, j, :], func=AF.Square),
             reads=[("xT", j)], writes=[("rl", 0)])
        S.op("dve", lambda e: e.tensor_tensor(out=rl[:, 1, :], in0=rl[:, 1, :], in1=rl[:, 0, :], op=ALU.add),
             reads=[("rl", 0), ("rl", 1)], writes=[("rl", 1)])

    def resid_proj(self, base, ntiles_per_chunk, nkc, rhs_fn, rres, gate, cond):
        S, t = self.S, self.t
        ps, xT = t["ps"], t["xT"]
        if ntiles_per_chunk == 0:
            for jp in range(8):
                w, wr, wn = self.wreq(base + jp)
                for jj in range(2):
                    j = jp * 2 + jj
                    b = self.bank()
                    S.op("pe", self.mmgroup(ps[:, b, :], [(w[:, kc, jj * 128:(jj + 1) * 128], rhs_fn(kc)) for kc in range(nkc)]),
                         reads=[wr] + rres, writes=[("ps", b)])
                    S.op("dve", lambda e, j=j, b=b: e.scalar_tensor_tensor(
                        out=xT[:, j, :], in0=ps[:, b, :], scalar=gate[:, j, cond:cond + 1], in1=xT[:, j, :],
                        op0=ALU.mult, op1=ALU.add), reads=[("ps", b), ("xT", j)] + MRES, writes=[("xT", j)])
                    self.sq_acc(j)
                self.wrel(wn)
        else:
            for j in range(16):
                w0, w0r, n0 = self.wreq(base + 2 * j)
                w1, w1r, n1 = self.wreq(base + 2 * j + 1)
                b = self.bank()
                pairs = [(w0[:, kc, :], rhs_fn(kc)) for kc in range(32)] + [(w1[:, kc, :], rhs_fn(32 + kc)) for kc in range(32)]
                S.op("pe", self.mmgroup(ps[:, b, :], pairs), reads=[w0r, w1r] + rres, writes=[("ps", b)])
                S.op("dve", lambda e, j=j, b=b: e.scalar_tensor_tensor(
                    out=xT[:, j, :], in0=ps[:, b, :], scalar=gate[:, j, cond:cond + 1], in1=xT[:, j, :],
                    op0=ALU.mult, op1=ALU.add), reads=[("ps", b), ("xT", j)] + MRES, writes=[("xT", j)])
                self.sq_acc(j)
                self.wrel(n0)
                self.wrel(n1)

    def ff1(self):
        S, t = self.S, self.t
        ps, h = t["ps"], t["h"]
        hres = [("h", c) for c in range(16)]
        hid, _ = self.R(0, 65536, BF16, "p (c n) -> p c n", c=64)
        rl = t["rl"]
        for i in range(32):
            w, wr, wn = self.wreq(I_F1 + i)
            for jj in range(2):
                c = i * 2 + jj
                b = self.bank()
                S.op("pe", self.mmgroup(ps[:, b, :], [(w[:, kc, jj * 128:(jj + 1) * 128], h[:, kc, 0:512]) for kc in range(16)]),
                     reads=[wr] + hres, writes=[("ps", b)])
                k = c % 2
                S.op("act", lambda e, k=k, b=b: e.activation(out=rl[:, k, :], in_=ps[:, b, :], func=AF.Relu),
                     reads=[("ps", b)], writes=[("rl", k)])
                eng = "pool" if c % 2 else "dve"
                S.op(eng, lambda e, k=k, c=c: e.tensor_tensor(out=hid[:, c, :], in0=rl[:, k, :], in1=rl[:, k, :], op=ALU.mult),
                     reads=[("rl", k)], writes=[("R", c)])
            self.wrel(wn)

    def xstg(self, k):
        t = self.t
        if k == 0:
            return t["xstage"][:], ["xstage"], "xstage"
        hf = t["h"][:].rearrange("p c n -> p (c n)")[:, (k - 1) * 4096:k * 4096].bitcast(F32)
        return hf, [("h", c) for c in range(16)], "hstg%d" % k

    def x_dma(self, src, t0, blk, k, eng="sp"):
        xs, xres, sem = self.xstg(k)
        self.S.op(eng, lambda e: e.dma_start(out=xs, in_=src[t0 + blk * 128:t0 + (blk + 1) * 128, :]),
                  writes=xres, dma=sem)

    def load_x_blk(self, src, t0, blk, k=0, do_dma=True):
        S, t = self.S, self.t
        xT, ps = t["xT"], t["ps"]
        xs, xres, _ = self.xstg(k)
        if do_dma:
            self.x_dma(src, t0, blk, k)
        st = t["xstat"]
        junk = t["rl"][:].rearrange("p a n -> p (a n)").bitcast(BF16)
        S.op("act", lambda e: e.activation(out=junk, in_=xs, func=AF.Square, accum_out=st[:, blk:blk + 1]),
             reads=xres, writes=[("rl", 0), ("rl", 1), ("xstat", blk)])
        S.op("act", lambda e: e.activation(out=st[:, 4 + blk:5 + blk], in_=st[:, blk:blk + 1], func=AF.Sqrt,
                                           bias=t["eps"][:, 0:1], scale=1.0 / D),
             reads=[("xstat", blk), "eps"], writes=[("xstat", 4 + blk)])
        S.op("dve", lambda e: e.reciprocal(st[:, 8 + blk:9 + blk], st[:, 4 + blk:5 + blk]),
             reads=[("xstat", 4 + blk)], writes=[("xstat", 8 + blk)])
        dg = t["diag"]
        S.op("dve", lambda e: e.tensor_scalar(out=dg[:], in0=t["c_ident"][:], scalar1=st[:, 8 + blk:9 + blk], scalar2=None,
                                              op0=ALU.mult), reads=[("xstat", 8 + blk), "ident"], writes=["diag"])
        for cg in range(4):
            b = self.bank()

            def f(e, b=b, cg=cg):
                ins = None
                for k in range(4):
                    c = cg * 4 + k
                    ins = e.transpose(ps[:, b, k * 128:(k + 1) * 128], xs[:, c * 128:(c + 1) * 128], t["c_ident"][:])
                return ins
            S.op("pe", f, reads=xres + ["ident"], writes=[("ps", b)])
            eng = self.alt2()
            out = xT[:, cg * 4:(cg + 1) * 4, blk * 128:(blk + 1) * 128]
            in_ = ps[:, b, :].rearrange("p (k n) -> p k n", k=4)
            S.op(eng, self.cp(eng, out, in_), reads=[("ps", b)], writes=[("xT", c) for c in range(cg * 4, cg * 4 + 4)])
        bd = self.bank()
        S.op("pe", lambda e: e.matmul(ps[:, bd, 0:128], t["ones_f"][:], dg[:], start=True, stop=True),
             reads=["diag", "ones_f"], writes=[("ps", bd)])
        rstd, rsres = self.R(32768 + 6144, 2048, F32)
        S.op("dve", lambda e: e.tensor_copy(rstd[:, blk * 128:(blk + 1) * 128], ps[:, bd, 0:128]),
             reads=[("ps", bd)], writes=rsres)

    def store_y(self, dst, t0, nxt=None):
        S, t = self.S, self.t
        ps = t["ps"]
        yT, _ = self.R(0, 32768, F32, "p (c n) -> p c n", c=16)
        for blk in range(4):
            ys, ysr = self.R(40960 + (blk % 2) * 8192, 8192, F32)
            for cg in range(4):
                b = self.bank()

                def f(e, b=b, cg=cg, blk=blk):
                    ins = None
                    for k in range(4):
                        c = cg * 4 + k
                        ins = e.transpose(ps[:, b, k * 128:(k + 1) * 128], yT[:, c, blk * 128:(blk + 1) * 128], t["c_ident"][:])
                    return ins
                S.op("pe", f, reads=sum([rrows(c * 2048, 2048) for c in range(cg * 4, cg * 4 + 4)], []) + ["ident"],
                     writes=[("ps", b)])
                eng = self.alt2()
                S.op(eng, self.cp(eng, ys[:, cg * 512:(cg + 1) * 512], ps[:, b, :]), reads=[("ps", b)], writes=ysr)
            S.op("sp", lambda e, blk=blk, ys=ys: e.dma_start(out=dst[t0 + blk * 128:t0 + (blk + 1) * 128, :], in_=ys),
                 reads=ysr, writes=[("y", id(dst), t0, blk)], dma="yst%d" % (blk % 2))
            if nxt is not None:
                self.load_x_blk(nxt[0], nxt[1], blk, k=(0, 1, 2, 0)[blk], do_dma=False)
                if blk == 0:
                    self.x_dma(nxt[0], nxt[1], 3, 0)

    def tile_phase2(self, sample, it, preloaded, nxt, halo_done=False, nxt_halo=None):
        t = self.t
        cond = 0 if sample else 1
        src = t["d_xs"] if sample else t["d_xp"]
        dst = t["o_ys"] if sample else t["o_yp"]
        ntok = TS if sample else TP
        t0 = it * T
        mod = t["mod"]
        sh1, g1 = mod[:, 0:16, :], mod[:, 32:48, :]
        sh1_lat = sh1
        sh2, g2 = mod[:, 48:64, :], mod[:, 80:96, :]
        if not preloaded:
            self.load_x(src, t0)
        if sample:
            self.load_rope(t0)
        self.norm(t["gm1"], sh1, cond, "h", have="rstd")
        if sample and not halo_done:
            self.halo(src, t0, ntok, t["gm1"], sh1, cond)
        self.q_proj(sample)
        if sample:
            self.conv(sample, it == 0, it == TS // T - 1, cond, offs=(16384, 19456, 21504, 57344),
                      side=self.q_stage_g(sample, t0, it))
            self.attention(sample)
            self.merge(57344)
        else:
            for _ in self.q_stage_g(sample, t0, it):
                pass
            self.attention(sample)
            self.conv(sample, it == 0, it == TS // T - 1, cond)
            self.merge()
        mg, mgr = self.R(0, 16384, BF16, "p (j n) -> p j n", j=16)
        self.resid_proj(I_WO, 0, 16, lambda kc: mg[:, kc, :], mgr, g1, cond)
        self.norm(t["gm2"], sh2, cond, "h", have="acc")
        self.ff1()
        hid, hidr = self.R(0, 65536, BF16, "p (c n) -> p c n", c=64)
        if nxt is not None:
            for blk in range(3):
                self.x_dma(nxt[0], nxt[1], blk, blk, eng="act")
        if nxt_halo is not None:
            self.halo(t["d_xs"], nxt_halo, TS, t["gm1"], sh1_lat, 0)
        self.resid_proj(I_F2, 2, 64, lambda kc: hid[:, kc, :], hidr, g2, cond)
        self.norm(t["fgm"], None, 0, "y", have="acc")
        self.store_y(dst, t0, nxt)

    def run(self):
        t = self.t
        self.prologue()
        mod = t["mod"]
        self.S.op("dve", lambda e: e.tensor_copy(t["fgm"][:, :, 0], t["c_fg"][:]), reads=["fg"], writes=["mods"])
        self.peng = "dve"
        nt = TS // T
        self.load_x(t["d_xs"], 0)
        for it in range(nt):
            self.load_rope(it * T)
            self.norm(t["gm1"], mod[:, 0:16, :], 0, "h", have="rstd")

            between = (t["d_xs"], (it + 1) * T) if it + 1 < nt else (t["d_xp"], 0)
            self.kv_phase1(it, between)
        self.peng = "pool"
        tiles = [(False, i) for i in range(TP // T)] + [(True, i) for i in range(TS // T)]
        for i, (sample, it) in enumerate(tiles):
            nxt = None
            if i + 1 < len(tiles):
                ns, ni = tiles[i + 1]
                nxt = (t["d_xs"] if ns else t["d_xp"], ni * T)
            nh_ = ni * T if (nxt is not None and ns) else None
            self.tile_phase2(sample, it, True, nxt, halo_done=(i > 0 and sample), nxt_halo=nh_)
        self.S.finish()


def declare(nc, es):
    t = {}

    def din(name, shape, dt=F32):
        t["d_" + name] = nc.dram_tensor(name, list(shape), dt, kind="ExternalInput").ap()

    def dout(name, shape):
        t["o_" + name] = nc.dram_tensor(name, list(shape), F32, kind="ExternalOutput").ap()
    din("xs", [TS, D]); din("xp", [TP, D]); din("ck", [256, 256]); din("cv", [256, 256])
    din("condT", [128, 16, 2]); din("w_mod", [D, 6 * D]); din("bmodT", [128, 96])
    din("n1g", [128, 16]); din("n2g", [128, 16]); din("fg", [128, 16])
    din("w_in", [D, INW]); din("gq_bc", [128, 128]); din("gk_bc", [128, 128]); din("convwT", [128, 8, 3])
    din("w_ao", [1024, D]); din("w_co", [1024, D]); din("w_o", [D, D]); din("w_ff1", [D, DFF]); din("w_ff2", [DFF, D])
    din("ident", [128, 128]); din("rope_cos", [TS, 64]); din("rope_sin", [TS, 64])
    dout("ys", [TS, D]); dout("yp", [TP, D]); dout("nk", [TP, 256]); dout("nv", [TP, 256])
    t["wscr"] = nc.dram_tensor("wscr", [NWT, 128, 4096], BF16, kind="Internal").ap()

    def sb(name, shape, dt):
        t[name] = es.enter_context(nc.sbuf_tensor("s_" + name, list(shape), dt))
    sb("R", [128, 32768], BF16)
    sb("KTs", [128, 2, TS], BF16); sb("KTc", [128, 2, 256], BF16)
    sb("Vs", [128, 32, 256], BF16); sb("Vc", [128, 2, 256], BF16)
    sb("xT", [128, 16, T], F32); sb("h", [128, 16, 514], BF16)
    sb("ring", [128, NS, 4096], BF16); sb("xstage", [128, D], F32)
    sb("c_ident", [128, 128], F32); sb("ident_b", [128, 128], BF16); sb("ones_b", [128, 128], BF16)
    sb("ones_mean", [128, 128], F32); sb("ones_f", [128, 128], F32)
    sb("c_condT", [128, 16, 2], F32); sb("silu", [128, 16, 2], F32); sb("c_bmodT", [128, 96], F32)
    sb("mod", [128, 96, 2], F32); sb("gm1", [128, 16, 2], F32); sb("gm2", [128, 16, 2], F32); sb("fgm", [128, 16, 1], F32)
    sb("c_n1g", [128, 16], F32); sb("c_n2g", [128, 16], F32); sb("c_fg", [128, 16], F32)
    sb("c_gq_bc", [128, 128], F32); sb("c_gk_bc", [128, 128], F32); sb("c_convwT", [128, 8, 3], F32)
    sb("rope_cos", [128, 4, 64], F32); sb("rope_sin", [128, 4, 64], F32)
    sb("negM", [128, 1], F32); sb("eps", [128, 1], F32); sb("small", [128, 8], F32); sb("small2", [128, 64], F32)
    sb("xstat", [128, 12], F32); sb("diag", [128, 128], F32); sb("hh", [128, 16, 2], BF16); sb("hsm", [128, 160], F32); sb("xh", [16, 256], F32); sb("rl", [128, 2, T], F32)
    t["ps"] = es.enter_context(nc.psum_tensor("ps", [128, 8, 512], F32))
    return t


def build():
    S0 = Sched(dry=True)
    P0 = Prog(None, S0, _DryTens(), None)
    P0.run()
    wseq = P0.wrec
    nc = bass.Bass("TRN2", target_bir_lowering=False)
    with ExitStack() as es:
        t = declare(nc, es)
        S = Sched(dry=False)
        P = Prog(nc, S, t, wseq)
        P.run()
        assert P.wi == len(wseq)
        sems = {name: es.enter_context(nc.semaphore(name)) for name in S.semcnt}
        block = es.enter_context(nc.Block())
        engmap = {"pe": block.tensor, "act": block.scalar, "dve": block.vector, "pool": block.gpsimd, "sp": block.sync}
        for en in ENGS:
            stream = S.streams[en]

            def body(eng, stream=stream):
                for waits, fn, sem, inc in stream:
                    for s, v in waits:
                        eng.wait_ge(sems[s], v)
                    if fn is not None:
                        fn(eng).then_inc(sems[sem], inc)
            engmap[en](body)
    return nc


class _DryAP:
    def __getitem__(self, k):
        return self

    def __getattr__(self, k):
        return lambda *a, **kw: self


class _DryTens(dict):
    def __missing__(self, k):
        return _DryAP()


def rope_tables():
    rows = TS // 64
    row = np.repeat(np.arange(rows), 64).astype(np.float32)
    col = np.tile(np.arange(64), rows).astype(np.float32)
    half = HD // 2
    inv = (np.float32(10000.0) ** (-np.arange(0, half, 2, dtype=np.float32) / np.float32(half))).astype(np.float32)
    ang = np.concatenate([row[:, None] * inv, col[:, None] * inv], axis=-1).astype(np.float32)
    return np.cos(ang).astype(np.float32), np.sin(ang).astype(np.float32)


def fm(v):
    return np.ascontiguousarray(np.asarray(v, np.float32).reshape(16, 128).T)


def make_in_maps(x_prompt, x_sample, cache_k, cache_v, c, c_ctx, w_mod, b_mod, norm1_g, norm2_g, w_in, q_gain, k_gain,
                 conv_w, w_attn_out, w_conv_out, w_o, w_ff1, w_ff2, final_g):
    f = lambda a: np.ascontiguousarray(np.asarray(a, np.float32))
    cos, sin = rope_tables()
    shared = {
        "w_mod": f(w_mod[0]), "bmodT": np.ascontiguousarray(f(b_mod[0]).reshape(96, 128).T),
        "n1g": fm(norm1_g[0]), "n2g": fm(norm2_g[0]), "fg": fm(final_g),
        "w_in": f(w_in[0]),
        "gq_bc": np.ascontiguousarray(np.broadcast_to(f(q_gain[0])[None, :], (128, 128))),
        "gk_bc": np.ascontiguousarray(np.broadcast_to(f(k_gain[0])[None, :], (128, 128))),
        "convwT": np.ascontiguousarray(f(conv_w[0]).reshape(3, 8, 128).transpose(2, 1, 0)),
        "w_ao": f(w_attn_out[0]), "w_co": f(w_conv_out[0]), "w_o": f(w_o[0]), "w_ff1": f(w_ff1[0]), "w_ff2": f(w_ff2[0]),
        "ident": np.eye(128, dtype=np.float32), "rope_cos": cos, "rope_sin": sin,
    }
    xs = f(x_sample); xp = f(x_prompt); ck = f(cache_k); cv = f(cache_v); cc = f(c); cx = f(c_ctx)
    maps = []
    for b in range(8):
        cond = np.stack([cc[b], cx], axis=0)
        condT = np.ascontiguousarray(cond.reshape(2, 16, 128).transpose(2, 1, 0))
        m = dict(shared)
        m.update({
            "xs": xs[b], "xp": np.ascontiguousarray(xp[4 * b:4 * b + 4].reshape(TP, D)),
            "ck": np.ascontiguousarray(ck[b, 0].reshape(256, 256)), "cv": np.ascontiguousarray(cv[b, 0].reshape(256, 256)),
            "condT": condT,
        })
        maps.append(m)
    return maps


_NC = None


def kernel(**inputs):
    global _NC
    maps = make_in_maps(**inputs)
    if _NC is None:
        _NC = build()
    res = run_bass_kernel_spmd(_NC, maps, core_ids=list(range(8)))
    r = res.results
    y_sample = np.stack([r[b]["ys"] for b in range(8)], axis=0).astype(np.float32)
    y_prompt = np.concatenate([r[b]["yp"].reshape(4, 256, D) for b in range(8)], axis=0).astype(np.float32)
    new_k = np.concatenate([r[b]["nk"].reshape(4, 1, 256, NKV, HD) for b in range(8)], axis=0).astype(np.float32)
    new_v = np.concatenate([r[b]["nv"].reshape(4, 1, 256, NKV, HD) for b in range(8)], axis=0).astype(np.float32)
    return (y_prompt, y_sample, new_k, new_v)
```
